# Optimizing a Trainium2 kernel written in Bass

```python
import jax, jax.numpy as jnp
from jax import lax
import numpy as np


D_MODEL = 1024
BATCH = 1
SEQ = 16384
DEPTH = 2
DEC_BATCH = 8
DEC_SEQ = 64
PAST_LEN = 1024

CHUNK = 64
N_A = DEPTH // 2
N_B = DEPTH - N_A
D_CONV = D_MODEL
CONV_WIDTH = 31
N_HEADS = 16
N_KV_HEADS = 2
GROUP = N_HEADS // N_KV_HEADS
HEAD_DIM = 64
ATTN_DIM = N_HEADS * HEAD_DIM
KV_DIM = N_KV_HEADS * HEAD_DIM
WINDOW = 128
WINDOW_CHUNKS = WINDOW // CHUNK
EPS = 1e-6
NEG_INF = -1e30

kernel_name = 'streaming_conformer_swa_sink_yoco'


def rmsnorm(x, g):
    xf = x.astype(jnp.float32)
    y = xf * lax.rsqrt(jnp.mean(xf * xf, axis=-1, keepdims=True) + EPS)
    return (y * g.astype(jnp.float32)).astype(x.dtype)


def layernorm(x, g, b):
    xf = x.astype(jnp.float32)
    mu = jnp.mean(xf, axis=-1, keepdims=True)
    xc = xf - mu
    y = xc * lax.rsqrt(jnp.mean(xc * xc, axis=-1, keepdims=True) + EPS)
    return (y * g.astype(jnp.float32) + b.astype(jnp.float32)).astype(x.dtype)


def alibi_slopes():
    h = jnp.arange(1, N_HEADS + 1, dtype=jnp.float32)
    return (2.0 ** (-8.0 * h / N_HEADS)).reshape(N_KV_HEADS, GROUP)


def conformer_conv_layer(x, hist, pre_g, w_in, b_in, w_dw, b_dw, ln_g, ln_b, w_out, b_out, post_g):
    h = rmsnorm(x, pre_g)
    z = h @ w_in + b_in
    a, gl, gate = jnp.split(z, 3, axis=-1)
    u = a * jax.nn.sigmoid(gl)
    u_hist = jnp.concatenate([hist, u], axis=1)
    c = lax.conv_general_dilated(u_hist, w_dw[:, None, :], (1,), 'VALID',
                                 dimension_numbers=('NWC', 'WIO', 'NWC'),
                                 feature_group_count=D_CONV) + b_dw
    c = jax.nn.silu(layernorm(c, ln_g, ln_b)) * jax.nn.silu(gate)
    y = c @ w_out + b_out
    return x + rmsnorm(y, post_g), u_hist[:, -(CONV_WIDTH - 1):]


def sink_alibi_attention(q, k, v, q_pos, k_pos, sinks):
    s = jnp.einsum('bnqkgd,bnskd->bnkgqs', q.astype(jnp.float32) * (HEAD_DIM ** -0.5),
                   k.astype(jnp.float32))
    qp = q_pos[:, :, None]
    kp = k_pos[:, None, :]
    dist = jnp.abs(qp - kp).astype(jnp.float32)
    dchunk = qp // CHUNK - kp // CHUNK
    valid = (kp >= 0) & (dchunk >= 0) & (dchunk <= WINDOW_CHUNKS)
    s = s - alibi_slopes()[None, None, :, :, None, None] * dist[None, :, None, None]
    s = jnp.where(valid[None, :, None, None], s, NEG_INF)
    sink = sinks.astype(jnp.float32).reshape(N_KV_HEADS, GROUP)[None, None, :, :, None, None]
    m = jnp.maximum(jnp.max(s, axis=-1, keepdims=True), sink)
    p = jnp.exp(s - m)
    denom = jnp.sum(p, axis=-1, keepdims=True) + jnp.exp(sink - m)
    o = jnp.einsum('bnkgqs,bnskd->bnqkgd', p / denom, v.astype(jnp.float32))
    return o.astype(q.dtype)


def banded_context(k, v):
    B, T = k.shape[:2]
    nc = T // CHUNK
    pad = WINDOW_CHUNKS * CHUNK
    kp = jnp.pad(k, ((0, 0), (pad, 0), (0, 0), (0, 0))).reshape(B, nc + WINDOW_CHUNKS, CHUNK, N_KV_HEADS, HEAD_DIM)
    vp = jnp.pad(v, ((0, 0), (pad, 0), (0, 0), (0, 0))).reshape(B, nc + WINDOW_CHUNKS, CHUNK, N_KV_HEADS, HEAD_DIM)
    kb = jnp.concatenate([kp[:, j:j + nc] for j in range(WINDOW_CHUNKS + 1)], axis=2)
    vb = jnp.concatenate([vp[:, j:j + nc] for j in range(WINDOW_CHUNKS + 1)], axis=2)
    q_pos = jnp.arange(T, dtype=jnp.int32).reshape(nc, CHUNK)
    k_pos = ((jnp.arange(nc, dtype=jnp.int32)[:, None] - WINDOW_CHUNKS) * CHUNK
             + jnp.arange((WINDOW_CHUNKS + 1) * CHUNK, dtype=jnp.int32)[None, :])
    return kb, vb, q_pos, k_pos


def sample_context(k_all, v_all, n_new):
    L = k_all.shape[1]
    q_pos = (PAST_LEN + jnp.arange(n_new, dtype=jnp.int32))[None, :]
    k_pos = (PAST_LEN + n_new - L + jnp.arange(L, dtype=jnp.int32))[None, :]
    return k_all[:, None], v_all[:, None], q_pos, k_pos


def swa_layer(x, ctx, pre_g, w_in, sinks, w_out, post_g):
    kb, vb, q_pos, k_pos = ctx
    B, T = x.shape[:2]
    h = rmsnorm(x, pre_g)
    z = h @ w_in
    q, gate = jnp.split(z, 2, axis=-1)
    q = q.reshape(B, q_pos.shape[0], q_pos.shape[1], N_KV_HEADS, GROUP, HEAD_DIM)
    o = sink_alibi_attention(q, kb, vb, q_pos, k_pos, sinks).reshape(B, T, ATTN_DIM)
    y = (o * jax.nn.silu(gate)) @ w_out
    return x + rmsnorm(y, post_g)


def setup_inputs(seed: int = 0) -> dict:
    key = jax.random.key(seed)
    ks = jax.random.split(key, 24)
    f32 = jnp.float32

    def nrm(k, shape, scale):
        return jax.random.normal(k, shape, f32) * scale

    rows = min(WINDOW, PAST_LEN)
    return {
        'x_prompt': nrm(ks[0], (BATCH, SEQ, D_MODEL), 1.0),
        'x_sample': nrm(ks[1], (DEC_BATCH, DEC_SEQ, D_MODEL), 1.0),
        'state_conv': nrm(ks[2], (N_A, DEC_BATCH, CONV_WIDTH - 1, D_CONV), 0.5),
        'cache_k': nrm(ks[3], (DEC_BATCH, rows, N_KV_HEADS, HEAD_DIM), 1.0),
        'cache_v': nrm(ks[4], (DEC_BATCH, rows, N_KV_HEADS, HEAD_DIM), 1.0),
        'a_pre_g': 1.0 + nrm(ks[5], (N_A, D_MODEL), 0.05),
        'a_w_in': nrm(ks[6], (N_A, D_MODEL, 3 * D_CONV), D_MODEL ** -0.5),
        'a_b_in': nrm(ks[7], (N_A, 3 * D_CONV), 0.02),
        'a_w_dw': nrm(ks[8], (N_A, CONV_WIDTH, D_CONV), CONV_WIDTH ** -0.5),
        'a_b_dw': nrm(ks[9], (N_A, D_CONV), 0.02),
        'a_ln_g': 1.0 + nrm(ks[10], (N_A, D_CONV), 0.05),
        'a_ln_b': nrm(ks[11], (N_A, D_CONV), 0.02),
        'a_w_out': nrm(ks[12], (N_A, D_CONV, D_MODEL), D_CONV ** -0.5),
        'a_b_out': nrm(ks[13], (N_A, D_MODEL), 0.02),
        'a_post_g': 1.0 + nrm(ks[14], (N_A, D_MODEL), 0.05),
        'kv_g': 1.0 + nrm(ks[15], (D_MODEL,), 0.05),
        'w_kv': nrm(ks[16], (D_MODEL, 2 * KV_DIM), D_MODEL ** -0.5),
        'b_pre_g': 1.0 + nrm(ks[17], (N_B, D_MODEL), 0.05),
        'b_w_in': nrm(ks[18], (N_B, D_MODEL, 2 * ATTN_DIM), D_MODEL ** -0.5),
        'b_sinks': nrm(ks[19], (N_B, N_HEADS), 0.5),
        'b_w_out': nrm(ks[20], (N_B, ATTN_DIM, D_MODEL), ATTN_DIM ** -0.5),
        'b_post_g': 1.0 + nrm(ks[21], (N_B, D_MODEL), 0.05),
    }


def reference(x_prompt, x_sample, state_conv, cache_k, cache_v,
              a_pre_g, a_w_in, a_b_in, a_w_dw, a_b_dw, a_ln_g, a_ln_b, a_w_out, a_b_out, a_post_g,
              kv_g, w_kv, b_pre_g, b_w_in, b_sinks, b_w_out, b_post_g):

    def trunk(x, conv_hist, k_prefix, v_prefix):
        B, T = x.shape[:2]
        new_hist = []
        ctx = None
        k_all = None
        v_all = None
        for layer in range(DEPTH):
            if layer < N_A:
                i = layer
                x, h = conformer_conv_layer(x, conv_hist[i], a_pre_g[i], a_w_in[i], a_b_in[i],
                                            a_w_dw[i], a_b_dw[i], a_ln_g[i], a_ln_b[i],
                                            a_w_out[i], a_b_out[i], a_post_g[i])
                new_hist.append(h)
                if layer == N_A - 1:
                    kv = rmsnorm(x, kv_g) @ w_kv
                    k = kv[..., :KV_DIM].reshape(B, T, N_KV_HEADS, HEAD_DIM)
                    v = kv[..., KV_DIM:].reshape(B, T, N_KV_HEADS, HEAD_DIM)
                    if k_prefix is None:
                        k_all, v_all = k, v
                        ctx = banded_context(k, v)
                    else:
                        k_all = jnp.concatenate([k_prefix.astype(k.dtype), k], axis=1)
                        v_all = jnp.concatenate([v_prefix.astype(v.dtype), v], axis=1)
                        ctx = sample_context(k_all, v_all, T)
            else:
                j = layer - N_A
                x = swa_layer(x, ctx, b_pre_g[j], b_w_in[j], b_sinks[j], b_w_out[j], b_post_g[j])
        return x, jnp.stack(new_hist, axis=0), k_all[:, -WINDOW:], v_all[:, -WINDOW:]

    zero_hist = jnp.zeros((N_A, x_prompt.shape[0], CONV_WIDTH - 1, D_CONV), x_prompt.dtype)
    y_prompt, conv_p, k_p, v_p = trunk(x_prompt, zero_hist, None, None)
    y_sample, conv_s, k_s, v_s = trunk(x_sample, state_conv.astype(x_sample.dtype), cache_k, cache_v)
    return (y_prompt, y_sample, conv_p, conv_s, k_p, v_p, k_s, v_s)
```

```python
import numpy as np
import concourse.bass as bass
import concourse.mybir as mybir
from concourse.bass_utils import run_bass_kernel_spmd

F32 = mybir.dt.float32
BF16 = mybir.dt.bfloat16
I32 = mybir.dt.int32
AF = mybir.ActivationFunctionType
ALU = mybir.AluOpType

NCORES = 8
D = 1024
KC = 8
WT = 256
EPS = 1e-6
NPT = 8
XROWS = 256 + 2048 + 128
YROWS = 2048 + 64
NEG = -30000.0
CONV_INTERLEAVE = False
DEBUG_STOP = None
NOARENA = ("d_dx", "dx", "dring", "dout", "dscr", "bg", "dw")
DEBUG_FLAGS = set()
PROFILE = False


class _Stop(Exception):
    pass


def _chk(name):
    if DEBUG_STOP is not None and name == DEBUG_STOP:
        raise _Stop()


class Prog:
    def __init__(self, nc):
        self.nc = nc
        self.h = {}
        self.sem = {}
        self.cnt = {}
        self.isdma = {}
        self.waited = {}
        self.lw = {}
        self.rd = {}

    def eng(self, name, handle, compute=True):
        self.h[name] = handle
        if compute:
            self.sem[name] = self.nc.alloc_semaphore("s_" + name)
            self.cnt[name] = 0
            self.isdma[name] = False

    def dsem(self, name):
        self.sem[name] = self.nc.alloc_semaphore("d_" + name)
        self.cnt[name] = 0
        self.isdma[name] = True

    def _wait(self, e, reads, writes):
        deps = {}
        for k in list(reads) + list(writes):
            if k in self.lw:
                p, n = self.lw[k]
                deps[p] = max(deps.get(p, 0), n)
        for k in writes:
            for p, n in self.rd.get(k, {}).items():
                deps[p] = max(deps.get(p, 0), n)
        for p, n in deps.items():
            if self.isdma[p]:
                if self.waited.get((e, p), 0) >= n:
                    continue
                n = self.cnt[p]
            if p == e and e == "pe":
                continue
            if self.waited.get((e, p), 0) < n:
                self.h[e].wait_ge(self.sem[p], n)
                self.waited[(e, p)] = n

    def _mark(self, prod, n, reads, writes):
        for k in writes:
            self.lw[k] = (prod, n)
            self.rd[k] = {}
        for k in reads:
            self.rd.setdefault(k, {})[prod] = n

    def op(self, e, fn, reads=(), writes=()):
        self._wait(e, reads, writes)
        ins = fn(self.h[e])
        ins.then_inc(self.sem[e], 1)
        self.cnt[e] += 1
        self._mark(e, self.cnt[e], reads, writes)

    def pe(self, fns, reads=(), writes=()):
        self._wait("pe", reads, writes)
        for f in fns[:-1]:
            f(self.h["pe"])
        ins = fns[-1](self.h["pe"])
        ins.then_inc(self.sem["pe"], 1)
        self.cnt["pe"] += 1
        self._mark("pe", self.cnt["pe"], reads, writes)

    def dma(self, q, ds, out, in_, reads=(), writes=()):
        self._wait(q, reads, writes)
        self.h[q].dma_start(out=out, in_=in_).then_inc(self.sem[ds], 16)
        self.cnt[ds] += 16
        self._mark(ds, self.cnt[ds], reads, writes)

    def barrier(self, skip=()):
        for e in self.h:
            for p in self.sem:
                if p == e and e == "pe":
                    continue
                if any(p.startswith(x) for x in skip):
                    continue
                n = self.cnt[p]
                if n > 0 and self.waited.get((e, p), 0) < n:
                    self.h[e].wait_ge(self.sem[p], n)
                    self.waited[(e, p)] = n

    def final_wait(self, q, names):
        for p in names:
            n = self.cnt[p]
            if n > 0:
                self.h[q].wait_ge(self.sem[p], n)


def build_program():
    nc = bass.Bass("TRN2", target_bir_lowering=False)
    P = Prog(nc)
    P.eng("pe", nc.tensor)
    P.eng("act", nc.scalar)
    P.eng("dve", nc.vector)
    P.eng("pool", nc.gpsimd)
    P.eng("sp", nc.sync, compute=False)
    for i in range(4):
        P.dsem("dx%d" % i)
        P.dsem("dring%d" % i)
    for i in range(4):
        P.dsem("du4%d" % i)
    for nm in ["dw%d%s" % (i, q) for i in range(6) for q in ("sp", "pool", "act")] + ["bg%d%s" % (i, q) for i in range(3) for q in ("sp", "pool", "act")] + ["dscr", "dscr0", "dscr1", "dout", "dmisc"]:
        P.dsem(nm)

    def din(name, shape):
        return nc.dram_tensor(name, list(shape), F32, kind="ExternalInput").ap()

    def dout(name, shape):
        return nc.dram_tensor(name, list(shape), F32, kind="ExternalOutput").ap()

    xin = din("xin", [XROWS, D])
    hist = din("hist", [32, D])
    ck = din("ck", [128, 128])
    cv = din("cv", [128, 128])
    flag_d = din("flag", [128, 1])
    w_in0 = din("w_in0", [D, 3 * D])
    w_out0 = din("w_out0", [D, D])
    w_kv = din("w_kv", [D, 256])
    w_in1 = din("w_in1p", [D, 2 * D])
    w_out1 = din("w_out1p", [D, D])
    smallp = din("smallp", [128, 96])
    wdw_d = din("wdw_st", [128, 256])
    rows_d = din("rows", [2, D])
    pg_d = din("pg", [2, 128, D])
    yout = dout("yout", [YROWS, D])
    convo = dout("convo", [2, 32, D])
    kvo = dout("kvo", [4, 128, 128])
    wscr = nc.dram_tensor("wscr", [12, 128, KC * 256], BF16, kind="Internal").ap()

    base = [16512]
    LIMIT = 16512 + 212864

    sb_off = {}

    def sb(name, shape, dt, at=None):
        n = int(np.prod(shape[1:])) * (4 if dt in (F32, I32) else 2)
        n = (n + 31) // 32 * 32
        if at is None:
            off = base[0]
            base[0] += n
            assert base[0] <= LIMIT, (name, base[0])
        else:
            off = at[0]
            at[0] += n
            assert at[0] <= LIMIT, (name, at[0])
        sb_off[name] = off
        return nc.alloc_sbuf_tensor_at(name, list(shape), dt, offset=off)

    Win0 = sb("Win0", [128, KC, 3 * D], BF16)
    Wout0 = sb("Wout0", [128, KC, D], BF16)
    Wkv = sb("Wkv", [128, KC, 256], BF16)
    Wc = sb("Wc", [128, 256, 32], BF16)
    biasT = [sb("biasA", [128, 16, 128], BF16), sb("biasB", [128, 16, 128], BF16)]
    pg = [sb("pg0", [128, D], F32), sb("pg1", [128, D], F32)]
    barow = sb("barow", [1, D], BF16)
    borow = sb("borow", [1, D], BF16)
    idb = sb("idb", [128, 128], BF16)
    idf = sb("idf", [128, 128], F32)
    ones = sb("ones", [128, 256], BF16)
    sp_ = sb("smallp", [128, 96], F32)
    spd = sb("spd", [128, 64], F32)
    kT = sb("kT", [128, 128 + WT], BF16)
    Vx = sb("Vx", [128, 3, 2, 128], BF16)
    UL = 32 + WT + 32
    Uc = sb("Uc", [128, KC, UL], BF16)
    ring = [sb("ring%d" % i, [128, KC, 256], BF16) for i in range(4)]
    ringf = [nc.alloc_sbuf_tensor_at("ringf%d" % i, [128, 1024], F32, offset=sb_off["ring%d" % i]) for i in range(3)]
    bstr = [nc.alloc_sbuf_tensor_at("bstr%d" % i, [128, 1024], BF16, offset=sb_off["ring3"] + 2048 * i) for i in range(2)]
    xs = [sb("xs%d" % i, [128, D], F32) for i in range(4)]
    hb = [sb("hb%d" % i, [128, D], BF16) for i in range(2)]
    junk = sb("junk", [128, D], BF16)
    hT = sb("hT", [128, KC, WT], BF16)
    n1T = sb("n1T", [128, KC, WT], BF16)
    stat = sb("stat", [128, 64], F32)
    mhalf = sb("mhalf", [128, WT], F32)
    flag = sb("flag", [128, 1], F32)
    epsc = sb("epsc", [128, 1], F32)
    arena0 = base[0]
    a0 = [arena0]
    cs = hT
    U4 = sb("U4", [128, KC, 4, WT + 32], BF16, a0)
    sgate = sb("sgate", [128, KC, WT], BF16, a0)
    cb = sb("cb", [128, KC, WT], BF16, a0)
    csq = sb("csq", [128, KC, WT], BF16, a0)
    th = [sb("th%d" % i, [128, WT], F32, a0) for i in range(2)]
    lnm = sb("lnm", [128, WT], F32, a0)
    lnv = sb("lnv", [128, WT], F32, a0)
    lnA = sb("lnA", [128, WT], F32, a0)
    lnB = sb("lnB", [128, WT], F32, a0)
    lnt = [[sb("lnt%d_%d" % (i, j), [128, WT], F32, a0) for j in range(2)] for i in range(2)]
    lns = [sb("lns%d" % i, [128, WT], BF16, a0) for i in range(2)]
    lnt4 = [lnt[0][0], lnt[0][1], lnt[1][0], lnt[1][1]]
    lns4 = lns + [nc.alloc_sbuf_tensor_at("lnsx%d" % i, [128, WT], BF16, offset=sb_off["th0"] + 512 * i) for i in range(2)]
    ytmp = sb("ytmp", [128, D], F32, a0)
    ufin = sb("ufin", [128, KC, 32], F32, a0)
    a1 = [arena0]
    qT = sb("qT", [128, KC, WT], BF16, a1)
    sg2 = sb("sg2", [128, KC, WT], BF16, a1)
    og = sb("og", [128, KC, WT], BF16, a1)
    PT = [[sb("PT%d_%d" % (bb, i), [128, 8, 128], BF16, a1) for i in range(4)] for bb in range(2)]
    dsum = sb("dsum", [128, 2, 128], F32, a1)
    rdenb = [sb("rden%d" % i, [128, 2, 128], F32, a1) for i in range(2)]
    onrmb = [sb("onrm%d" % i, [128, 2, 128], F32, a1) for i in range(2)]
    ktok = sb("ktok", [128, 128], F32, a1)
    vtok = sb("vtok", [128, 128], F32, a1)
    a2 = [arena0]
    stg = [sb("stg%d" % i, [128, 1024], F32, a2) for i in range(6)]
    bst = [sb("bst%d" % i, [128, 1024], BF16, a2) for i in range(2)]
    idi = sb("idi", [128, 128], I32, a2)
    dqk = sb("dqk", [128, 128], F32, a2)
    dA = sb("dA", [128, 128], F32, a2)
    dB = sb("dB", [128, 128], F32, a2)
    mkA = sb("mkA", [128, 128], F32, a2)
    mkB = sb("mkB", [128, 128], F32, a2)
    m32 = sb("m32", [128, 32], F32, a2)
    m32t = sb("m32t", [128, 32], F32, a2)
    wst = sb("wst", [128, 256], F32, a2)
    rowst = sb("rowst", [1, 2, D], F32, a2)
    a3 = [arena0]
    histf = sb("histf", [32, D], F32, a3)
    histb = sb("histb", [32, D], BF16, a3)
    ckf = sb("ckf", [128, 128], F32, a3)
    ckb = sb("ckb", [128, 128], BF16, a3)
    cvf = sb("cvf", [128, 128], F32, a3)

    tp = nc.alloc_psum_tensor("tp", [128, KC, 128], BF16)
    pbig = nc.alloc_psum_tensor("pbig", [128, 3584], F32)

    def psk(lo, hi):
        return ["ps%d" % i for i in range(lo // 512, (hi + 511) // 512)]

    def pjc(col, w):
        return pbig[:, col:col + w], psk(col, col + w)

    def pjvo(s, lo, hi):
        sel, part = s // 3, s % 3
        return pjc(1024 * sel + 256 * part + lo, hi - lo)

    def pjv(s, w):
        sel, part = s // 3, s % 3
        return pjc(1024 * sel + 256 * part, w)

    BIN, BDW, LNG, LNB, G0, GKV, G1, SNK = 0, 24, 32, 40, 48, 56, 64, 72
    GA, BGLH, GQ, ESK = 0, 8, 16, 24

    scope_stack = []

    def scope(name):
        if not PROFILE:
            return
        if scope_stack:
            nm, sid = scope_stack.pop()
            nc.leave_named_scope(nm, sid, False)
        if name is not None:
            sid, _ = nc.enter_named_scope(name, False)
            scope_stack.append((name, sid))

    scope("setup_const")
    P.dma("sp", "dmisc", sp_[:], smallp[:, :], writes=["smallp"])
    P.dma("sp", "dmisc", flag[:], flag_d[:, :], writes=["flag"])
    P.dma("sp", "dmisc", wst[:], wdw_d[:, :], writes=["wst"])
    P.dma("sp", "dmisc", rowst[:], rows_d.rearrange("(o r) d -> o r d", o=1), writes=["rowst"])
    P.dma("sp", "dmisc", pg[0][:], pg_d[0], writes=["pg0"])
    P.dma("sp", "dmisc", pg[1][:], pg_d[1], writes=["pg1"])
    P.op("dve", lambda e: e.tensor_single_scalar(spd[:, GA:GA + 8], sp_[:, G0:G0 + 8], 0.5, ALU.mult),
         reads=["smallp"], writes=["spd_a"])
    P.op("dve", lambda e: e.tensor_single_scalar(spd[:, BGLH:BGLH + 8], sp_[:, BIN + 8:BIN + 16], 0.5, ALU.mult),
         reads=["smallp"], writes=["spd_b"])
    P.op("dve", lambda e: e.tensor_single_scalar(spd[:, GQ:GQ + 8], sp_[:, G1:G1 + 8], 0.125, ALU.mult),
         reads=["smallp"], writes=["spd_c"])
    P.op("act", lambda e: e.activation(spd[:, ESK:ESK + 8], sp_[:, SNK:SNK + 8], AF.Exp), reads=["smallp"], writes=["spd_d"])
    P.op("dve", lambda e: e.tensor_single_scalar(barow[:], rowst[:, 0, :], 0.5, ALU.mult), reads=["rowst"], writes=["barow"])
    P.op("dve", lambda e: e.tensor_copy(borow[:], rowst[:, 1, :]), reads=["rowst"], writes=["borow"])

    P.op("pool", lambda e: e.iota(idi[:], [[1, 128]], base=0, channel_multiplier=-1), writes=["idi"])
    P.op("dve", lambda e: e.tensor_copy(dqk[:], idi[:]), reads=["idi"], writes=["dqk"])
    P.op("dve", lambda e: e.tensor_single_scalar(idf[:], dqk[:], 0.0, ALU.is_equal), reads=["dqk"], writes=["idf"])
    P.op("dve", lambda e: e.tensor_copy(idb[:], idf[:]), reads=["idf"], writes=["idb"])
    P.op("dve", lambda e: e.memset(ones[:], 1.0), writes=["ones"])
    P.op("dve", lambda e: e.memset(mhalf[:], -0.5), writes=["mhalf"])
    P.op("dve", lambda e: e.memset(epsc[:], EPS), writes=["epsc"])
    P.op("pool", lambda e: e.memset(Uc[:], 0.0), writes=["Uc%d" % c for c in range(KC)])
    P.op("pool", lambda e: e.memset(kT[:], 0.0), writes=["kT"])
    P.op("pool", lambda e: e.memset(Vx[:], 0.0), writes=["Vx0", "Vx1", "Vx2"])
    P.op("pool", lambda e: e.memset(stat[:], 0.0), writes=["st%d" % i for i in range(64)])
    P.op("dve", lambda e: e.tensor_single_scalar(m32[:], dqk[:, 0:32], 0.0, ALU.is_equal), reads=["dqk"], writes=["m32"])
    for g in range(1, 4):
        P.op("dve", lambda e, g=g: e.tensor_single_scalar(m32t[:], dqk[:, 0:32], -32.0 * g, ALU.is_equal),
             reads=["dqk"], writes=["m32t"])
        P.op("dve", lambda e: e.tensor_tensor(m32[:], m32[:], m32t[:], ALU.add), reads=["m32", "m32t"], writes=["m32"])
    P.op("dve", lambda e: e.tensor_tensor(Wc[:], wst[:].unsqueeze(2).to_broadcast([128, 256, 32]),
                                          m32[:].unsqueeze(1).to_broadcast([128, 256, 32]), ALU.mult),
         reads=["wst", "m32"], writes=["Wc"])
    P.op("dve", lambda e: e.tensor_single_scalar(dA[:], dqk[:], 128.0, ALU.add), reads=["dqk"], writes=["dA"])
    P.op("dve", lambda e: e.tensor_single_scalar(dB[:], dqk[:], -1.0, ALU.mult), reads=["dqk"], writes=["dB"])
    P.op("dve", lambda e: e.tensor_tensor(dB[:], dB[:], dqk[:], ALU.max), reads=["dB", "dqk"], writes=["dB"])
    P.op("dve", lambda e: e.memset(mkA[:], 0.0), writes=["mkA"])
    P.op("dve", lambda e: e.memset(mkB[:], 0.0), writes=["mkB"])
    P.op("dve", lambda e: e.memset(mkA[0:64, 64:128], NEG), reads=["mkA"], writes=["mkA"])
    P.op("dve", lambda e: e.memset(mkB[64:128, 0:64], NEG), reads=["mkB"], writes=["mkB"])
    for hh in range(16):
        sl = -(2.0 ** (-(hh + 1) / 2.0))
        P.op("dve", lambda e, hh=hh, sl=sl: e.scalar_tensor_tensor(biasT[0][:, hh, :], dA[:], sl, mkA[:], ALU.mult, ALU.add),
             reads=["dA", "mkA"], writes=["biasA"])
        P.op("dve", lambda e, hh=hh, sl=sl: e.scalar_tensor_tensor(biasT[1][:, hh, :], dB[:], sl, mkB[:], ALU.mult, ALU.add),
             reads=["dB", "mkB"], writes=["biasB"])
    scope("setup_w")
    piece_i = [0]
    scr_i = [0]

    def wpiece(src_rows, ncols, gscal, mul, dst=None, dstkey=None, scr=None):
        i = piece_i[0] % 6
        q = ("sp", "pool", "act")[piece_i[0] % 3]
        piece_i[0] += 1
        P.dma(q, "dw%d%s" % (i, q), stg[i][:, 0:ncols], src_rows, writes=["stg%d" % i])
        if dst is not None:
            tgt, tkey = dst, dstkey
        else:
            j = scr_i[0] % 2
            scr_i[0] += 1
            tgt, tkey = bst[j][:, 0:ncols], "bst%d" % j
        eng = "dve" if piece_i[0] % 2 == 0 else "act"
        rk = ["stg%d" % i, "smallp", "spd_a", "spd_c"]
        if eng == "dve":
            if gscal is None:
                P.op("dve", lambda e: e.tensor_copy(tgt, stg[i][:, 0:ncols]), reads=rk, writes=[tkey])
            else:
                P.op("dve", lambda e: e.tensor_single_scalar(tgt, stg[i][:, 0:ncols], gscal, ALU.mult), reads=rk, writes=[tkey])
        else:
            if gscal is None:
                P.op("act", lambda e: e.activation(tgt, stg[i][:, 0:ncols], AF.Identity), reads=rk, writes=[tkey])
            else:
                P.op("act", lambda e: e.activation(tgt, stg[i][:, 0:ncols], AF.Identity, scale=gscal), reads=rk, writes=[tkey])
        if scr is not None:
            p0, kc = scr
            P.dma("sp", "dscr", wscr[p0:p0 + 4, :, kc * 256:(kc + 1) * 256].rearrange("q p n -> p q n"),
                  bst[j][:, 0:ncols].rearrange("p (q n) -> p q n", q=4), reads=["bst%d" % j], writes=["wscr"])

    for kc in range(KC):
        rows = slice(kc * 128, (kc + 1) * 128)
        for cb_ in range(3):
            gs = spd[:, GA + kc:GA + kc + 1] if cb_ == 0 else sp_[:, G0 + kc:G0 + kc + 1]
            wpiece(w_in0[rows, cb_ * 1024:(cb_ + 1) * 1024], 1024, gs, None,
                   dst=Win0[:, kc, cb_ * 1024:(cb_ + 1) * 1024], dstkey="Win0")
    for kc in range(KC):
        rows = slice(kc * 128, (kc + 1) * 128)
        wpiece(w_out0[rows, :], 1024, None, None, dst=Wout0[:, kc, :], dstkey="Wout0")
        wpiece(w_kv[rows, :], 256, sp_[:, GKV + kc:GKV + kc + 1], None, dst=Wkv[:, kc, :], dstkey="Wkv")
    bg_list = []
    for kc in range(KC):
        rows = slice(kc * 128, (kc + 1) * 128)
        bg_list.append((w_in1[rows, 0:1024], spd[:, GQ + kc:GQ + kc + 1], (0, kc)))
        bg_list.append((w_in1[rows, 1024:2048], sp_[:, G1 + kc:G1 + kc + 1], (4, kc)))
        bg_list.append((w_out1[rows, :], None, (8, kc)))
    bg_state = {"dma": 0, "cast": 0}

    def bg_dma():
        j = bg_state["dma"]
        if j >= len(bg_list):
            return
        bg_state["dma"] += 1
        src_rows, _, _ = bg_list[j]
        i = j % 3
        q = ("sp", "pool", "act")[j % 3]
        P.dma(q, "bg%d%s" % (i, q), ringf[i][:], src_rows, writes=["ring%d" % i])

    def bg_pump(n=1):
        for _ in range(n):
            j = bg_state["cast"]
            if j >= len(bg_list):
                return
            while bg_state["dma"] < min(j + 3, len(bg_list)):
                bg_dma()
            bg_state["cast"] += 1
            _, gscal, (p0, kc) = bg_list[j]
            i = j % 3
            jb = j % 2
            eng = "dve" if j % 2 == 0 else "act"
            rk = ["ring%d" % i, "smallp", "spd_c"]
            wk = ["ring3", "bstr%d" % jb]
            if eng == "dve":
                if gscal is None:
                    P.op("dve", lambda e: e.tensor_copy(bstr[jb][:], ringf[i][:]), reads=rk, writes=wk)
                else:
                    P.op("dve", lambda e: e.tensor_single_scalar(bstr[jb][:], ringf[i][:], gscal, ALU.mult), reads=rk, writes=wk)
            else:
                if gscal is None:
                    P.op("act", lambda e: e.activation(bstr[jb][:], ringf[i][:], AF.Identity), reads=rk, writes=wk)
                else:
                    P.op("act", lambda e: e.activation(bstr[jb][:], ringf[i][:], AF.Identity, scale=gscal), reads=rk, writes=wk)
            P.dma("sp", "dscr%d" % jb, wscr[p0:p0 + 4, :, kc * 256:(kc + 1) * 256].rearrange("q p n -> p q n"),
                  bstr[jb][:].rearrange("p (q n) -> p q n", q=4), reads=["bstr%d" % jb], writes=["wscr%d" % jb])
            if bg_state["dma"] < len(bg_list):
                bg_dma()

    P.barrier()
    stopped = DEBUG_STOP == 'setup'

    stc = [0]

    def newstat():
        c = stc[0] % 64
        stc[0] += 1
        return c, "st%d" % c

    ring_seq = []
    ring_issued = [0]

    def ring_ensure(upto):
        while ring_issued[0] < min(upto + 1, len(ring_seq)):
            gi = ring_issued[0]
            pcs = ring_seq[gi]
            sl = gi % 4
            P.dma("sp", "dring%d" % sl, ring[sl][:], wscr[pcs].rearrange("p (kc n) -> p kc n", kc=KC),
                  reads=["wscr", "wscr0", "wscr1"], writes=["ring%d" % sl])
            ring_issued[0] += 1

    def rstd_from(cols_keys):
        (c0, k0) = cols_keys[0]
        if len(cols_keys) == 2:
            (c1, k1) = cols_keys[1]
            cs_, ks_ = newstat()
            P.op("dve", lambda e: e.tensor_tensor(stat[:, cs_:cs_ + 1], stat[:, c0:c0 + 1], stat[:, c1:c1 + 1], ALU.add),
                 reads=[k0, k1], writes=[ks_])
            c0, k0 = cs_, ks_
        cm, km = newstat()
        P.op("dve", lambda e: e.tensor_scalar(stat[:, cm:cm + 1], stat[:, c0:c0 + 1], 1.0 / D, EPS, ALU.mult, ALU.add),
             reads=[k0], writes=[km])
        cr, kr = newstat()
        P.op("pool", lambda e: e.tensor_tensor(stat[:, cr:cr + 1], stat[:, cm:cm + 1], mhalf[:, 0:1], ALU.pow),
             reads=[km, "mhalf"], writes=[kr])
        return cr, kr

    hbi = [0]

    def norm_sq(xslot):
        xk = "xs%d" % xslot
        c, k = newstat()
        P.op("act", lambda e: e.activation(junk[:], xs[xslot][:], AF.Square, accum_out=stat[:, c:c + 1]),
             reads=[xk], writes=["junk", k])
        return rstd_from([(c, k)])

    def norm_scale(xslot, rs):
        cr, kr = rs
        i = hbi[0] % 2
        hbi[0] += 1
        P.op("act", lambda e: e.activation(hb[i][:], xs[xslot][:], AF.Identity, scale=stat[:, cr:cr + 1]),
             reads=["xs%d" % xslot, kr], writes=["hb%d" % i])
        return i

    def norm_front(xslot):
        return norm_scale(xslot, norm_sq(xslot))

    def norm_back(i, dstT, dkey, b):
        P.pe([lambda e, kc=kc: e.transpose(tp[:, kc, :], hb[i][:, kc * 128:(kc + 1) * 128], idb[:]) for kc in range(KC)],
             reads=["hb%d" % i, "idb"], writes=["tp"])
        P.op("dve", lambda e: e.tensor_copy(dstT[:, :, b * 128:(b + 1) * 128], tp[:]), reads=["tp"], writes=[dkey])

    def norm_transpose(xslot, dstT, dkey, b):
        norm_back(norm_front(xslot), dstT, dkey, b)

    def post_norm_residual(xslot, pgi, ycol):
        c, k = newstat()
        P.op("act", lambda e: e.activation(junk[:, 0:1024], pbig[:, ycol:ycol + 1024], AF.Square, accum_out=stat[:, c:c + 1]),
             reads=psk(ycol, ycol + 1024), writes=["junk", k])
        cks = [(c, k)]
        cr, kr = rstd_from(cks)
        for nh in range(2):
            lo = ycol + nh * 512
            P.op("dve", lambda e, nh=nh, lo=lo: e.scalar_tensor_tensor(pbig[:, lo:lo + 512], pbig[:, lo:lo + 512],
                                                                      stat[:, cr:cr + 1], pg[pgi][:, nh * 512:(nh + 1) * 512],
                                                                      ALU.mult, ALU.mult),
                 reads=psk(lo, lo + 512) + [kr, "pg%d" % pgi], writes=psk(lo, lo + 512))
            P.op("dve", lambda e, nh=nh, lo=lo: e.tensor_tensor(xs[xslot][:, nh * 512:(nh + 1) * 512],
                                                               xs[xslot][:, nh * 512:(nh + 1) * 512], pbig[:, lo:lo + 512],
                                                               ALU.add),
                 reads=psk(lo, lo + 512) + ["xs%d" % xslot], writes=["xs%d" % xslot])

    tiles = [dict(kind="halo", row0=0, W=256, l1=False, fin=None)]
    for t in range(NPT):
        tiles.append(dict(kind="prompt", row0=256 + 256 * t, W=256, l1=True, fin=(0 if t == NPT - 1 else None),
                          yrow=256 * t))
    tiles.append(dict(kind="sample", row0=2304, W=128, l1=True, fin=1, yrow=2048))
    for tl in tiles:
        if tl["l1"]:
            tl["ring0"] = len(ring_seq)
            ring_seq.extend(list(range(8)) + [8, 9, 10, 11] * (tl["W"] // 128))

    xcnt = [0]
    xslot_of = {}

    def issue_x(ti):
        tl = tiles[ti]
        for b in range(tl["W"] // 128):
            sl = xcnt[0] % 4
            xcnt[0] += 1
            xslot_of[(ti, b)] = sl
            P.dma("sp", "dx%d" % sl, xs[sl][:], xin[tl["row0"] + b * 128: tl["row0"] + (b + 1) * 128, :],
                  writes=["xs%d" % sl])

    u4c = [0]

    s1_done = set()

    def stage1(ti):
        s1_done.add(ti)
        for b in range(tiles[ti]["W"] // 128):
            norm_transpose(xslot_of[(ti, b)], hT, "hT", b)

    s1_pending = {}

    def stage1_front(ti):
        s1_done.add(ti)
        s1_pending[ti] = [norm_front(xslot_of[(ti, b)]) for b in range(tiles[ti]["W"] // 128)]

    def stage1_back(ti):
        for b, i in enumerate(s1_pending.pop(ti)):
            norm_back(i, hT, "hT", b)

    def run_tiles():
      issue_x(0)
      for ti, tl in enumerate(tiles):
        W = tl["W"]
        nb = W // 128
        halo = tl["kind"] == "halo"
        c0 = 128 if halo else 0
        c0a = 96 if halo else 0
        blocks = [1] if halo else list(range(W // 128))
        fin = tl["fin"]
        if ti + 1 < len(tiles):
            issue_x(ti + 1)
        if tl["kind"] == "sample":
            P.dma("sp", "dmisc", histf[:], hist[:, :], writes=["histf"])
            P.dma("sp", "dmisc", ckf[:], ck[:, :], writes=["ckf"])
            P.dma("sp", "dmisc", cvf[:], cv[:, :], writes=["cvf"])
            P.op("dve", lambda e: e.tensor_copy(histb[:], histf[:]), reads=["histf"], writes=["histb"])
            P.op("dve", lambda e: e.tensor_copy(ckb[:], ckf[:]), reads=["ckf"], writes=["ckb"])
            P.pe([lambda e, kc=kc: e.transpose(tp[:, kc, 0:32], histb[:, kc * 128:(kc + 1) * 128], idb[0:32, 0:32])
                  for kc in range(KC)], reads=["histb", "idb"], writes=["tp"])
            P.op("dve", lambda e: e.tensor_copy(Uc[:, :, 0:32], tp[:, :, 0:32]), reads=["tp"],
                 writes=["Uc%d" % c for c in range(KC)])
            P.pe([lambda e: e.transpose(tp[:, 0, :], ckb[:], idb[:])], reads=["ckb", "idb"], writes=["tp"])
            P.op("dve", lambda e: e.tensor_copy(kT[:, 0:128], tp[:, 0, :]), reads=["tp"], writes=["kT"])
            P.op("dve", lambda e: e.tensor_copy(Vx[:, 0, :, 0:64], cvf[:].rearrange("p (k d) -> p k d", k=2)),
                 reads=["cvf"], writes=["Vx0"])
            P.op("dve", lambda e: e.memset(Vx[:, 0, :, 64:128], 1.0), reads=["Vx0"], writes=["Vx0"])
            P.dma("sp", "dout", kvo[2, 0:64, :], ck[64:128, :])
            P.dma("sp", "dout", kvo[3, 0:64, :], cv[64:128, :])
            P.barrier()

        scope("T%d_s1" % ti)
        if ti not in s1_done:
            stage1(ti)

        _chk('t%d_s1' % ti)
        scope("T%d_s2" % ti)
        def inproj(c, parts):
            sel = c % 2
            for part in parts:
                cx = c0 if part == 2 else c0a
                pv, pk = pjvo(3 * sel + part, cx, W)
                col0 = part * D + c * 128
                fns = [lambda e, kc=kc, pv=pv, col0=col0, cx=cx: e.matmul(pv, Win0[:, kc, col0:col0 + 128], hT[:, kc, cx:W],
                                                                   start=(kc == 0), stop=(part != 0 and kc == KC - 1))
                       for kc in range(KC)]
                if part == 0:
                    fns.append(lambda e, pv=pv, cx=cx: e.matmul(pv, barow[0:1, c * 128:(c + 1) * 128], ones[0:1, cx:W],
                                                         start=False, stop=True))
                P.pe(fns, reads=["hT", "Win0", "barow", "ones"], writes=pk)

        def evac_u(c):
            sel = c % 2
            pa, pak = pjvo(3 * sel + 0, c0a, W)
            pgl, pglk = pjvo(3 * sel + 1, c0a, W)
            i = c % 2
            P.op("act", lambda e: e.activation(th[i][:, c0a:W], pgl, AF.Tanh, bias=spd[:, BGLH + c:BGLH + c + 1], scale=0.5),
                 reads=pglk + ["spd_b"], writes=["th%d" % i])
            P.op("dve", lambda e: e.scalar_tensor_tensor(Uc[:, c, 32 + c0a:32 + W], th[i][:, c0a:W], 1.0, pa, ALU.add, ALU.mult),
                 reads=pak + ["th%d" % i], writes=["Uc%d" % c])
            if fin is not None:
                lo, hi = (W - 32, W) if fin == 0 else (32, 64)
                P.op("dve", lambda e: e.scalar_tensor_tensor(ufin[:, c, :], th[i][:, lo:hi], 1.0,
                                                             pbig[:, 1024 * sel + lo:1024 * sel + hi], ALU.add, ALU.mult),
                     reads=pak + ["th%d" % i], writes=["ufin"])

        def evac_gate(c):
            sel = c % 2
            pgt, pgtk = pjvo(3 * sel + 2, c0, W)
            P.op("act", lambda e: e.activation(sgate[:, c, c0:W], pgt, AF.Silu, bias=sp_[:, BIN + 16 + c:BIN + 17 + c]),
                 reads=pgtk + ["smallp"], writes=["sgate%d" % c])

        def conv(c):
            sel = c % 2
            order = [(fb, m) for fb in range(4) for m in range(8)]
            if CONV_INTERLEAVE:
                order = [(fb, m) for m in range(8) for fb in range(4)]
            fns = [lambda e, fb=fb, m=m: e.matmul(pbig[32 * fb:32 * fb + 32, 1024 * sel + 768 + c0:1024 * sel + 768 + W],
                                                  Wc[:, (c * 4 + fb) * 8 + m, :], U4[:, c, fb, 2 + m + c0:2 + m + W],
                                                  start=(m == 0), stop=(m == 7), tile_position=(0, 32 * fb))
                   for (fb, m) in order]
            pk = psk(1024 * sel + 768, 1024 * sel + 768 + W)
            P.pe(fns, reads=["U4_%d_%d_%d" % (c // 4, g, fb) for g in range(4) for fb in range(4)] + ["Wc"], writes=pk)
            pcv = pbig[:, 1024 * sel + 768 + c0:1024 * sel + 768 + W]
            P.op("act", lambda e: e.activation(cb[:, c, c0:W], pcv, AF.Identity, bias=sp_[:, BDW + c:BDW + c + 1]),
                 reads=pk + ["smallp"], writes=["cb%d" % c])
            P.op("act", lambda e: e.activation(csq[:, c, c0:W], pcv, AF.Square, bias=sp_[:, BDW + c:BDW + c + 1]),
                 reads=pk + ["smallp"], writes=["csq%d" % c])

        def stats_all():
            P.pe([lambda e, c=c: e.matmul(pbig[:, 2048 + c0:2048 + W], ones[:, 0:128], cb[:, c, c0:W], start=(c == 0),
                                          stop=(c == KC - 1)) for c in range(KC)],
                 reads=["cb%d" % c for c in range(KC)] + ["ones"], writes=psk(2048, 2560))
            P.pe([lambda e, c=c: e.matmul(pbig[:, 2304 + c0:2304 + W], ones[:, 0:128], csq[:, c, c0:W], start=(c == 0),
                                          stop=(c == KC - 1)) for c in range(KC)],
                 reads=["csq%d" % c for c in range(KC)] + ["ones"], writes=psk(2048, 2560))

        s2step = [0]

        def s2_pump():
            s2step[0] += 1
            if s2step[0] % (2 if ti == 0 else 4) == 0:
                bg_pump(1)

        L = W + 32
        qn = [0]

        def shift_copies(h):
            for g in range(4):
                for fb in range(4):
                    q = ("sp", "du4%d" % (2 * h)) if qn[0] % 2 == 0 else ("pool", "du4%d" % (2 * h + 1))
                    qn[0] += 1
                    P.dma(q[0], q[1], U4[32 * g:32 * g + 32, 4 * h:4 * h + 4, fb, 0:L],
                          Uc[32 * fb:32 * fb + 32, 4 * h:4 * h + 4, 8 * g:8 * g + L],
                          reads=["Uc%d" % c for c in range(4 * h, 4 * h + 4)], writes=["U4_%d_%d_%d" % (h, g, fb)])

        inproj(0, (0, 1))
        for c in range(KC):
            if c + 1 < KC:
                inproj(c + 1, (0, 1))
            evac_u(c)
            s2_pump()
            if c == 3:
                shift_copies(0)
        shift_copies(1)
        for c in range(KC):
            inproj(c, (2,))
            evac_gate(c)
            s2_pump()
        for c in range(KC):
            conv(c)
            s2_pump()
        stats_all()
        uck = ["Uc%d" % c for c in range(KC)]
        if halo:
            P.op("dve", lambda e: e.tensor_single_scalar(Uc[:, :, 2:32], Uc[:, :, W + 2:W + 32], flag[:, 0:1], ALU.mult),
                 reads=uck + ["flag"], writes=uck)
        else:
            P.op("dve", lambda e: e.tensor_copy(Uc[:, :, 2:32], Uc[:, :, W + 2:W + 32]), reads=uck, writes=uck)
        if fin is not None:
            P.pe([lambda e, c=c: e.transpose(pbig[0:32, 2560 + c * 128:2560 + (c + 1) * 128], ufin[:, c, :], idf[:])
                  for c in range(KC)], reads=["ufin", "idf"], writes=psk(2560, 3584))
            P.op("dve", lambda e: e.tensor_copy(ytmp[0:32, :], pbig[0:32, 2560:3584]), reads=psk(2560, 3584), writes=["ytmp"])
            P.dma("sp", "dout", convo[fin], ytmp[0:32, :], reads=["ytmp"])

        _chk('t%d_s2' % ti)
        scope("T%d_s3" % ti)
        st1 = pbig[:, 2048 + c0:2048 + W]
        st2 = pbig[:, 2304 + c0:2304 + W]
        stk = psk(2048, 2560)
        P.op("dve", lambda e: e.tensor_single_scalar(lnm[:, c0:W], st1, 1.0 / D, ALU.mult), reads=stk, writes=["lnm"])
        P.op("dve", lambda e: e.tensor_tensor(lnv[:, c0:W], lnm[:, c0:W], lnm[:, c0:W], ALU.mult), reads=["lnm"], writes=["lnv"])
        P.op("dve", lambda e: e.scalar_tensor_tensor(lnA[:, c0:W], st2, 1.0 / D, lnv[:, c0:W], ALU.mult, ALU.subtract),
             reads=stk + ["lnv"], writes=["lnA"])
        P.op("act", lambda e: e.activation(lnv[:, c0:W], lnA[:, c0:W], AF.Ln, bias=epsc[:, 0:1]), reads=["lnA", "epsc"],
             writes=["lnv"])
        P.op("act", lambda e: e.activation(lnA[:, c0:W], lnv[:, c0:W], AF.Exp, scale=-0.5), reads=["lnv"], writes=["lnA"])
        P.op("dve", lambda e: e.scalar_tensor_tensor(lnB[:, c0:W], lnm[:, c0:W], -1.0, lnA[:, c0:W], ALU.mult, ALU.mult),
             reads=["lnm", "lnA"], writes=["lnB"])
        def ln_front(c):
            i = c % 4
            P.op("pool", lambda e: e.tensor_tensor(lnt4[i][:, c0:W], cb[:, c, c0:W], lnA[:, c0:W], ALU.mult),
                 reads=["cb%d" % c, "lnA"], writes=["lntq%d" % i])
            P.op("dve", lambda e: e.tensor_tensor(lnt4[i][:, c0:W], lnt4[i][:, c0:W], lnB[:, c0:W], ALU.add),
                 reads=["lntq%d" % i, "lnB"], writes=["lntq%d" % i])
            P.op("act", lambda e: e.activation(lns4[i][:, c0:W], lnt4[i][:, c0:W], AF.Silu,
                                               bias=sp_[:, LNB + c:LNB + c + 1], scale=sp_[:, LNG + c:LNG + c + 1]),
                 reads=["lntq%d" % i, "smallp"], writes=["lnsq%d" % i] + (["th0"] if i >= 2 else []))

        def ln_back(c):
            i = c % 4
            P.op("dve", lambda e: e.tensor_tensor(cs[:, c, c0:W], lns4[i][:, c0:W], sgate[:, c, c0:W], ALU.mult),
                 reads=["lnsq%d" % i, "sgate%d" % c, "hT"] + (["th0"] if i >= 2 else []), writes=["cs%d" % c])
            P.pe([lambda e, b=b, nh=nh: e.matmul(pbig[:, 1024 * b + nh * 512:1024 * b + (nh + 1) * 512],
                                                 cs[:, c, b * 128:(b + 1) * 128], Wout0[:, c, nh * 512:(nh + 1) * 512],
                                                 start=(c == 0), stop=False) for b in blocks for nh in range(2)],
                 reads=["cs%d" % c, "Wout0"], writes=psk(0, 1024 * nb))

        LA = 3
        for c in range(LA):
            ln_front(c)
        for c in range(KC):
            if c + LA < KC:
                ln_front(c + LA)
            ln_back(c)
            if c % 2 == 1:
                bg_pump(1)
        P.pe([lambda e, b=b, nh=nh: e.matmul(pbig[:, 1024 * b + nh * 512:1024 * b + (nh + 1) * 512], ones[0:1, 0:128],
                                             borow[0:1, nh * 512:(nh + 1) * 512], start=False, stop=True)
              for b in blocks for nh in range(2)], reads=["borow", "ones"], writes=psk(0, 1024 * nb))

        _chk('t%d_s3' % ti)
        scope("T%d_s45" % ti)
        for b in blocks:
            post_norm_residual(xslot_of[(ti, b)], 0, 1024 * b)
        rs_ = {b: norm_sq(xslot_of[(ti, b)]) for b in blocks}
        hi_ = {b: norm_scale(xslot_of[(ti, b)], rs_[b]) for b in blocks}
        for b in blocks:
            norm_back(hi_[b], n1T, "n1T", b)
        pkv, pkk = pjc(c0, W - c0)
        P.pe([lambda e, kc=kc: e.matmul(pkv, Wkv[:, kc, 0:128], n1T[:, kc, c0:W], start=(kc == 0), stop=(kc == KC - 1))
              for kc in range(KC)], reads=["n1T", "Wkv"], writes=pkk)
        P.op("act", lambda e: e.activation(kT[:, 128 + c0:128 + W], pkv, AF.Identity), reads=pkk, writes=["kT"])
        for b in blocks:
            pvv, pvk = pjc(1024 + 512 * b, 128)
            P.pe([lambda e, kc=kc: e.matmul(pvv, n1T[:, kc, b * 128:(b + 1) * 128], Wkv[:, kc, 128:256],
                                            start=(kc == 0), stop=(kc == KC - 1)) for kc in range(KC)],
                 reads=["n1T", "Wkv"], writes=pvk)
            vk = "Vx%d" % (1 + b)
            pv3 = pvv.rearrange("p (k d) -> p k d", k=2)
            if halo:
                P.op("dve", lambda e: e.tensor_single_scalar(Vx[:, 1 + b, :, 0:64], pv3, flag[:, 0:1], ALU.mult),
                     reads=pvk + ["flag"], writes=[vk])
                P.op("dve", lambda e: e.tensor_copy(Vx[:, 1 + b, :, 64:128],
                                                    flag[:, 0:1].unsqueeze(2).to_broadcast([128, 2, 64])),
                     reads=["flag", vk], writes=[vk])
            else:
                P.op("dve", lambda e: e.tensor_copy(Vx[:, 1 + b, :, 0:64], pv3), reads=pvk, writes=[vk])
                P.op("dve", lambda e: e.memset(Vx[:, 1 + b, :, 64:128], 1.0), reads=[vk], writes=[vk])
        _chk('t%d_s45' % ti)
        P.barrier(skip=NOARENA)

        if fin is not None:
            b = nb - 1
            pvv, pvk = pjc(1024 + 512 * b, 128)
            P.op("dve", lambda e: e.tensor_copy(vtok[:], pvv), reads=pvk, writes=["vtok"])
            pkt, pktk = pjc(512, 128)
            P.pe([lambda e, kc=kc: e.matmul(pkt, n1T[:, kc, b * 128:(b + 1) * 128], Wkv[:, kc, 0:128],
                                            start=(kc == 0), stop=(kc == KC - 1)) for kc in range(KC)],
                 reads=["n1T", "Wkv"], writes=pktk)
            P.op("dve", lambda e: e.tensor_copy(ktok[:], pkt), reads=pktk, writes=["ktok"])
            if fin == 0:
                P.dma("sp", "dout", kvo[0], ktok[:], reads=["ktok"])
                P.dma("sp", "dout", kvo[1], vtok[:], reads=["vtok"])
            else:
                P.dma("sp", "dout", kvo[2, 64:128, :], ktok[0:64, :], reads=["ktok"])
                P.dma("sp", "dout", kvo[3, 64:128, :], vtok[0:64, :], reads=["vtok"])

        if tl["l1"]:
            bg_pump(len(bg_list))
            r0 = tl["ring0"]
            scope("T%d_s6" % ti)
            if ti + 1 < len(tiles):
                stage1_front(ti + 1)
            for oc in range(16):
                gi = r0 + oc // 2
                ring_ensure(gi + 3)
                sl = gi % 4
                pv, pk = pjc(512 * (oc % 4), W)
                P.pe([lambda e, kc=kc: e.matmul(pv, ring[sl][:, kc, (oc % 2) * 128:(oc % 2) * 128 + 128], n1T[:, kc, 0:W],
                                                start=(kc == 0), stop=(kc == KC - 1)) for kc in range(KC)],
                     reads=["n1T", "ring%d" % sl], writes=pk)
                if oc < 8:
                    P.op("dve", lambda e: e.tensor_copy(qT[:, oc, 0:W], pv), reads=pk, writes=["qT"])
                else:
                    P.op("act", lambda e: e.activation(sg2[:, oc - 8, 0:W], pv, AF.Silu), reads=pk, writes=["sg2"])
            if ti + 1 < len(tiles):
                stage1_back(ti + 1)
            _chk('t%d_s6' % ti)
            scope("T%d_s78" % ti)
            def att_scores_step(b, idx):
                kv, blk = idx // 2, idx % 2
                s0 = (idx % 2) * 1024
                S3 = pbig[:, s0:s0 + 1024].rearrange("p (g q) -> p g q", g=8)
                sk = psk(s0, s0 + 1024)
                kcols = kT[64 * kv:64 * kv + 64, blk * 128 + b * 128: blk * 128 + b * 128 + 128]
                fns = []
                for hf in range(2):
                    fns.append(lambda e, hf=hf: e.matmul(S3[:, hf * 4:(hf + 1) * 4, :], idb[:],
                                                         biasT[blk][:, kv * 8 + hf * 4:kv * 8 + hf * 4 + 4, :],
                                                         start=True, stop=False))
                for g in range(8):
                    fns.append(lambda e, g=g: e.matmul(S3[:, g, :], kcols, qT[64 * kv:64 * kv + 64, g, b * 128:(b + 1) * 128],
                                                       start=False, stop=(g % 4 == 3)))
                P.pe(fns, reads=["kT", "qT", "idb", "biasA", "biasB"], writes=sk)
                for hf in range(2):
                    P.op("act", lambda e, hf=hf: e.activation(PT[b % 2][idx][:, hf * 4:(hf + 1) * 4, :],
                                                              S3[:, hf * 4:(hf + 1) * 4, :], AF.Exp),
                         reads=sk, writes=["PT%d_%d" % (b % 2, idx)])

            def att_pv_step(b, jp):
                if True:
                    pb0 = 2048 if jp % 2 == 0 else (0 if b == nb - 1 else 2560)
                    o3 = pbig[:, pb0:pb0 + 256].rearrange("p (j q) -> p j q", j=2)
                    d3 = pbig[:, pb0 + 256:pb0 + 512].rearrange("p (j q) -> p j q", j=2)
                    pok = psk(pb0, pb0 + 512)
                    fns = []
                    for jj in range(2):
                        j = 2 * jp + jj
                        for kv in range(2):
                            for (c0_, v0_) in ((pb0, 0), (pb0 + 256, 64)):
                                for blk in range(2):
                                    fns.append(lambda e, jj=jj, j=j, kv=kv, blk=blk, c0_=c0_, v0_=v0_: e.matmul(
                                        pbig[64 * kv:64 * kv + 64, c0_ + jj * 128:c0_ + (jj + 1) * 128],
                                        Vx[:, b + blk, kv, v0_:v0_ + 64], PT[b % 2][kv * 2 + blk][:, j, :],
                                        start=(blk == 0), stop=(blk == 1), tile_position=(0, 64 * kv)))
                    P.pe(fns, reads=["Vx%d" % b, "Vx%d" % (b + 1)] + ["PT%d_%d" % (b % 2, i) for i in range(4)], writes=pok)
                    rb = jp % 2
                    for jj in range(2):
                        P.op("act", lambda e, jj=jj: e.activation(dsum[:, jj, :], d3[:, jj, :], AF.Ln,
                                                                  bias=spd[:, ESK + 2 * jp + jj:ESK + 2 * jp + jj + 1]),
                             reads=pok + ["spd_d"], writes=["dsum"])
                    P.op("act", lambda e: e.activation(rdenb[rb][:], dsum[:], AF.Exp, scale=-1.0), reads=["dsum"],
                         writes=["rden%d" % rb])
                    P.op("dve", lambda e: e.tensor_tensor(onrmb[rb][:], o3, rdenb[rb][:], ALU.mult),
                         reads=pok + ["rden%d" % rb], writes=["onrm%d" % rb])
                    P.op("pool", lambda e: e.tensor_tensor(og[:, 2 * jp:2 * jp + 2, b * 128:(b + 1) * 128], onrmb[rb][:],
                                                           sg2[:, 2 * jp:2 * jp + 2, b * 128:(b + 1) * 128], ALU.mult),
                         reads=["onrm%d" % rb, "sg2"], writes=["og%d" % b])

            def out_proj1_mm(b, ycol, nqs=(0, 1, 2, 3)):
                for nq in nqs:
                    gi = r0 + 8 + 4 * b + nq
                    ring_ensure(gi + 3)
                    sl = gi % 4
                    P.pe([lambda e, kc=kc: e.matmul(pbig[:, ycol + nq * 256:ycol + (nq + 1) * 256],
                                                    og[:, kc, b * 128:(b + 1) * 128], ring[sl][:, kc, :],
                                                    start=(kc == 0), stop=(kc == KC - 1)) for kc in range(KC)],
                         reads=["og%d" % b, "ring%d" % sl], writes=psk(ycol + nq * 256, ycol + (nq + 1) * 256))

            def out_proj1_post(b, ycol):
                xsl = xslot_of[(ti, b)]
                post_norm_residual(xsl, 1, ycol)
                nrow = 128 if tl["kind"] == "prompt" else 64
                P.dma("sp", "dout", yout[tl["yrow"] + b * 128: tl["yrow"] + b * 128 + nrow, :], xs[xsl][0:nrow, :],
                      reads=["xs%d" % xsl])

            for idx in range(4):
                att_scores_step(0, idx)
            ycols = [2560, 1024]
            for b in range(nb):
                for jp in range(4):
                    att_pv_step(b, jp)
                    if b + 1 < nb:
                        att_scores_step(b + 1, jp)
                    elif nb > 1:
                        out_proj1_mm(b - 1, ycols[b - 1], (jp,))
            out_proj1_mm(nb - 1, ycols[nb - 1])
            out_proj1_post(0, ycols[0])
            for b in range(1, nb):
                out_proj1_post(b, ycols[b])
            if ti + 1 < len(tiles) and tiles[ti + 1]["l1"]:
                ring_ensure(tiles[ti + 1]["ring0"] + 3)
        _chk('t%d_s8' % ti)
        scope("T%d_end" % ti)
        if tl["kind"] != "sample":
            P.op("dve", lambda e: e.tensor_copy(kT[:, 0:128], kT[:, W:W + 128]), reads=["kT"], writes=["kT"])
            P.op("dve", lambda e: e.tensor_copy(Vx[:, 0], Vx[:, nb]), reads=["Vx%d" % nb], writes=["Vx0"])
        P.barrier(skip=NOARENA)

    if not stopped:
        try:
            run_tiles()
        except _Stop:
            pass
    scope(None)
    P.barrier()
    P.final_wait("sp", [k for k in P.sem if P.isdma[k]])
    return nc


_CACHE = {}


def kernel(**inputs):
    f32 = np.float32
    g = lambda k: np.asarray(inputs[k], dtype=f32)
    x_prompt = g("x_prompt")[0]
    x_sample = g("x_sample")
    state_conv = g("state_conv")[0]
    cache_k = g("cache_k").reshape(8, 128, 128)
    cache_v = g("cache_v").reshape(8, 128, 128)
    a_b_in = g("a_b_in")[0]
    perm = np.concatenate([np.concatenate([np.arange(j * 64, j * 64 + 64), np.arange((8 + j) * 64, (8 + j) * 64 + 64)])
                           for j in range(8)])
    w_in1 = g("b_w_in")[0]
    w_in1p = np.ascontiguousarray(np.concatenate([w_in1[:, perm], w_in1[:, 1024 + perm]], axis=1))
    w_out1p = np.ascontiguousarray(g("b_w_out")[0][perm, :])
    fm = lambda v: np.ascontiguousarray(v.reshape(-1, 128).T)
    sinks = g("b_sinks")[0]
    sinks_fm = np.concatenate([np.tile(sinks[None, 0:8], (64, 1)), np.tile(sinks[None, 8:16], (64, 1))], axis=0)
    smallp = np.zeros((128, 96), f32)
    smallp[:, 0:24] = fm(a_b_in)
    smallp[:, 24:32] = fm(g("a_b_dw")[0])
    smallp[:, 32:40] = fm(g("a_ln_g")[0])
    smallp[:, 40:48] = fm(g("a_ln_b")[0])
    smallp[:, 48:56] = fm(g("a_pre_g")[0])
    smallp[:, 56:64] = fm(g("kv_g"))
    smallp[:, 64:72] = fm(g("b_pre_g")[0])
    smallp[:, 72:80] = sinks_fm
    wdw = g("a_w_dw")[0]
    wpad = np.concatenate([wdw, np.zeros((1, 1024), f32)], axis=0).reshape(4, 8, 32, 32)
    wdw_st = np.ascontiguousarray(wpad.transpose(0, 3, 2, 1).reshape(128, 256))
    rows = np.stack([a_b_in[0:1024], g("a_b_out")[0]], axis=0)
    pgb = np.stack([np.tile(g("a_post_g")[0][None, :], (128, 1)), np.tile(g("b_post_g")[0][None, :], (128, 1))], axis=0)
    common = dict(w_in0=np.ascontiguousarray(g("a_w_in")[0]), w_out0=np.ascontiguousarray(g("a_w_out")[0]),
                  w_kv=np.ascontiguousarray(g("w_kv")), w_in1p=w_in1p, w_out1p=w_out1p, smallp=smallp,
                  wdw_st=wdw_st, rows=np.ascontiguousarray(rows), pg=np.ascontiguousarray(pgb))
    in_maps = []
    for i in range(NCORES):
        xin = np.zeros((XROWS, D), f32)
        if i > 0:
            xin[0:256] = x_prompt[2048 * i - 256:2048 * i]
        xin[256:2304] = x_prompt[2048 * i:2048 * (i + 1)]
        xin[2304:2368] = x_sample[i]
        hist = np.zeros((32, D), f32)
        hist[2:32] = state_conv[i]
        m = dict(common)
        m.update(xin=xin, hist=hist, ck=np.ascontiguousarray(cache_k[i]), cv=np.ascontiguousarray(cache_v[i]),
                 flag=np.full((128, 1), 0.0 if i == 0 else 1.0, f32))
        in_maps.append(m)
    if "nc" not in _CACHE:
        _CACHE["nc"] = build_program()
    res = run_bass_kernel_spmd(_CACHE["nc"], in_maps, core_ids=list(range(NCORES)))
    R = res.results
    y_prompt = np.concatenate([R[i]["yout"][0:2048] for i in range(NCORES)], axis=0)[None].astype(f32)
    y_sample = np.stack([R[i]["yout"][2048:2112] for i in range(NCORES)], axis=0).astype(f32)
    conv_p = R[NCORES - 1]["convo"][0][2:32][None, None].astype(f32)
    conv_s = np.stack([R[i]["convo"][1][2:32] for i in range(NCORES)], axis=0)[None].astype(f32)
    k_p = R[NCORES - 1]["kvo"][0].reshape(1, 128, 2, 64).astype(f32)
    v_p = R[NCORES - 1]["kvo"][1].reshape(1, 128, 2, 64).astype(f32)
    k_s = np.stack([R[i]["kvo"][2].reshape(128, 2, 64) for i in range(NCORES)], axis=0).astype(f32)
    v_s = np.stack([R[i]["kvo"][3].reshape(128, 2, 64) for i in range(NCORES)], axis=0).astype(f32)
    return (y_prompt, y_sample, conv_p, conv_s, k_p, v_p, k_s, v_s)
```

```python
import numpy as np
import concourse.bass as bass
import concourse.mybir as mybir
from concourse.bass_utils import run_bass_kernel_spmd

F32 = mybir.dt.float32
BF16 = mybir.dt.bfloat16
I32 = mybir.dt.int32
AF = mybir.ActivationFunctionType
ALU = mybir.AluOpType

NCORES = 8
D = 1024
KC = 8
WT = 256
EPS = 1e-6
NPT = 8
XROWS = 256 + 2048 + 128
YROWS = 2048 + 64
NEG = -30000.0
CONV_INTERLEAVE = False
DEBUG_STOP = None
NOARENA = ("d_dx", "dx", "dring", "dout", "dscr", "bg", "dw")
DEBUG_FLAGS = set()
PROFILE = False


class _Stop(Exception):
    pass


def _chk(name):
    if DEBUG_STOP is not None and name == DEBUG_STOP:
        raise _Stop()


class Prog:
    def __init__(self, nc):
        self.nc = nc
        self.h = {}
        self.sem = {}
        self.cnt = {}
        self.isdma = {}
        self.waited = {}
        self.lw = {}
        self.rd = {}

    def eng(self, name, handle, compute=True):
        self.h[name] = handle
        if compute:
            self.sem[name] = self.nc.alloc_semaphore("s_" + name)
            self.cnt[name] = 0
            self.isdma[name] = False

    def dsem(self, name):
        self.sem[name] = self.nc.alloc_semaphore("d_" + name)
        self.cnt[name] = 0
        self.isdma[name] = True

    def _wait(self, e, reads, writes):
        deps = {}
        for k in list(reads) + list(writes):
            if k in self.lw:
                p, n = self.lw[k]
                deps[p] = max(deps.get(p, 0), n)
        for k in writes:
            for p, n in self.rd.get(k, {}).items():
                deps[p] = max(deps.get(p, 0), n)
        for p, n in deps.items():
            if self.isdma[p]:
                if self.waited.get((e, p), 0) >= n:
                    continue
                n = self.cnt[p]
            if p == e and e == "pe":
                continue
            if self.waited.get((e, p), 0) < n:
                self.h[e].wait_ge(self.sem[p], n)
                self.waited[(e, p)] = n

    def _mark(self, prod, n, reads, writes):
        for k in writes:
            self.lw[k] = (prod, n)
            self.rd[k] = {}
        for k in reads:
            self.rd.setdefault(k, {})[prod] = n

    def op(self, e, fn, reads=(), writes=()):
        self._wait(e, reads, writes)
        ins = fn(self.h[e])
        ins.then_inc(self.sem[e], 1)
        self.cnt[e] += 1
        self._mark(e, self.cnt[e], reads, writes)

    def pe(self, fns, reads=(), writes=()):
        self._wait("pe", reads, writes)
        for f in fns[:-1]:
            f(self.h["pe"])
        ins = fns[-1](self.h["pe"])
        ins.then_inc(self.sem["pe"], 1)
        self.cnt["pe"] += 1
        self._mark("pe", self.cnt["pe"], reads, writes)

    def dma(self, q, ds, out, in_, reads=(), writes=()):
        self._wait(q, reads, writes)
        self.h[q].dma_start(out=out, in_=in_).then_inc(self.sem[ds], 16)
        self.cnt[ds] += 16
        self._mark(ds, self.cnt[ds], reads, writes)

    def barrier(self, skip=()):
        for e in self.h:
            for p in self.sem:
                if p == e and e == "pe":
                    continue
                if any(p.startswith(x) for x in skip):
                    continue
                n = self.cnt[p]
                if n > 0 and self.waited.get((e, p), 0) < n:
                    self.h[e].wait_ge(self.sem[p], n)
                    self.waited[(e, p)] = n

    def final_wait(self, q, names):
        for p in names:
            n = self.cnt[p]
            if n > 0:
                self.h[q].wait_ge(self.sem[p], n)


def build_program():
    nc = bass.Bass("TRN2", target_bir_lowering=False)
    P = Prog(nc)
    P.eng("pe", nc.tensor)
    P.eng("act", nc.scalar)
    P.eng("dve", nc.vector)
    P.eng("pool", nc.gpsimd)
    P.eng("sp", nc.sync, compute=False)
    for i in range(4):
        P.dsem("dx%d" % i)
        P.dsem("dring%d" % i)
    for i in range(4):
        P.dsem("du4%d" % i)
    for nm in ["dw%d%s" % (i, q) for i in range(6) for q in ("sp", "pool", "act")] + ["bg%d%s" % (i, q) for i in range(3) for q in ("sp", "pool", "act")] + ["dscr", "dscr0", "dscr1", "dout", "dmisc"]:
        P.dsem(nm)

    def din(name, shape):
        return nc.dram_tensor(name, list(shape), F32, kind="ExternalInput").ap()

    def dout(name, shape):
        return nc.dram_tensor(name, list(shape), F32, kind="ExternalOutput").ap()

    xin = din("xin", [XROWS, D])
    hist = din("hist", [32, D])
    ck = din("ck", [128, 128])
    cv = din("cv", [128, 128])
    flag_d = din("flag", [128, 1])
    w_in0 = din("w_in0", [D, 3 * D])
    w_out0 = din("w_out0", [D, D])
    w_kv = din("w_kv", [D, 256])
    w_in1 = din("w_in1p", [D, 2 * D])
    w_out1 = din("w_out1p", [D, D])
    smallp = din("smallp", [128, 96])
    wdw_d = din("wdw_st", [128, 256])
    rows_d = din("rows", [2, D])
    pg_d = din("pg", [2, 128, D])
    yout = dout("yout", [YROWS, D])
    convo = dout("convo", [2, 32, D])
    kvo = dout("kvo", [4, 128, 128])
    wscr = nc.dram_tensor("wscr", [12, 128, KC * 256], BF16, kind="Internal").ap()

    base = [16512]
    LIMIT = 16512 + 212864

    sb_off = {}

    def sb(name, shape, dt, at=None):
        n = int(np.prod(shape[1:])) * (4 if dt in (F32, I32) else 2)
        n = (n + 31) // 32 * 32
        if at is None:
            off = base[0]
            base[0] += n
            assert base[0] <= LIMIT, (name, base[0])
        else:
            off = at[0]
            at[0] += n
            assert at[0] <= LIMIT, (name, at[0])
        sb_off[name] = off
        return nc.alloc_sbuf_tensor_at(name, list(shape), dt, offset=off)

    Win0 = sb("Win0", [128, KC, 3 * D], BF16)
    Wout0 = sb("Wout0", [128, KC, D], BF16)
    Wkv = sb("Wkv", [128, KC, 256], BF16)
    Wc = sb("Wc", [128, 256, 32], BF16)
    biasT = [sb("biasA", [128, 16, 128], BF16), sb("biasB", [128, 16, 128], BF16)]
    pg = [sb("pg0", [128, D], F32), sb("pg1", [128, D], F32)]
    barow = sb("barow", [1, D], BF16)
    borow = sb("borow", [1, D], BF16)
    idb = sb("idb", [128, 128], BF16)
    idf = sb("idf", [128, 128], F32)
    ones = sb("ones", [128, 256], BF16)
    sp_ = sb("smallp", [128, 96], F32)
    spd = sb("spd", [128, 64], F32)
    kT = sb("kT", [128, 128 + WT], BF16)
    Vx = sb("Vx", [128, 3, 2, 128], BF16)
    UL = 32 + WT + 32
    Uc = sb("Uc", [128, KC, UL], BF16)
    ring = [sb("ring%d" % i, [128, KC, 256], BF16) for i in range(4)]
    ringf = [nc.alloc_sbuf_tensor_at("ringf%d" % i, [128, 1024], F32, offset=sb_off["ring%d" % i]) for i in range(3)]
    bstr = [nc.alloc_sbuf_tensor_at("bstr%d" % i, [128, 1024], BF16, offset=sb_off["ring3"] + 2048 * i) for i in range(2)]
    xs = [sb("xs%d" % i, [128, D], F32) for i in range(4)]
    hb = [sb("hb%d" % i, [128, D], BF16) for i in range(2)]
    junk = sb("junk", [128, D], BF16)
    hT = sb("hT", [128, KC, WT], BF16)
    n1T = sb("n1T", [128, KC, WT], BF16)
    stat = sb("stat", [128, 64], F32)
    mhalf = sb("mhalf", [128, WT], F32)
    flag = sb("flag", [128, 1], F32)
    epsc = sb("epsc", [128, 1], F32)
    arena0 = base[0]
    a0 = [arena0]
    cs = hT
    U4 = sb("U4", [128, KC, 4, WT + 32], BF16, a0)
    sgate = sb("sgate", [128, KC, WT], BF16, a0)
    cb = sb("cb", [128, KC, WT], BF16, a0)
    csq = sb("csq", [128, KC, WT], BF16, a0)
    th = [sb("th%d" % i, [128, WT], F32, a0) for i in range(2)]
    lnm = sb("lnm", [128, WT], F32, a0)
    lnv = sb("lnv", [128, WT], F32, a0)
    lnA = sb("lnA", [128, WT], F32, a0)
    lnB = sb("lnB", [128, WT], F32, a0)
    lnt = [[sb("lnt%d_%d" % (i, j), [128, WT], F32, a0) for j in range(2)] for i in range(2)]
    lns = [sb("lns%d" % i, [128, WT], BF16, a0) for i in range(2)]
    lnt4 = [lnt[0][0], lnt[0][1], lnt[1][0], lnt[1][1]]
    lns4 = lns + [nc.alloc_sbuf_tensor_at("lnsx%d" % i, [128, WT], BF16, offset=sb_off["th0"] + 512 * i) for i in range(2)]
    ytmp = sb("ytmp", [128, D], F32, a0)
    ufin = sb("ufin", [128, KC, 32], F32, a0)
    a1 = [arena0]
    qT = sb("qT", [128, KC, WT], BF16, a1)
    sg2 = sb("sg2", [128, KC, WT], BF16, a1)
    og = sb("og", [128, KC, WT], BF16, a1)
    PT = [[sb("PT%d_%d" % (bb, i), [128, 8, 128], BF16, a1) for i in range(4)] for bb in range(2)]
    dsum = sb("dsum", [128, 2, 128], F32, a1)
    rdenb = [sb("rden%d" % i, [128, 2, 128], F32, a1) for i in range(2)]
    onrmb = [sb("onrm%d" % i, [128, 2, 128], F32, a1) for i in range(2)]
    ktok = sb("ktok", [128, 128], F32, a1)
    vtok = sb("vtok", [128, 128], F32, a1)
    a2 = [arena0]
    stg = [sb("stg%d" % i, [128, 1024], F32, a2) for i in range(6)]
    bst = [sb("bst%d" % i, [128, 1024], BF16, a2) for i in range(2)]
    idi = sb("idi", [128, 128], I32, a2)
    dqk = sb("dqk", [128, 128], F32, a2)
    dA = sb("dA", [128, 128], F32, a2)
    dB = sb("dB", [128, 128], F32, a2)
    mkA = sb("mkA", [128, 128], F32, a2)
    mkB = sb("mkB", [128, 128], F32, a2)
    m32 = sb("m32", [128, 32], F32, a2)
    m32t = sb("m32t", [128, 32], F32, a2)
    wst = sb("wst", [128, 256], F32, a2)
    rowst = sb("rowst", [1, 2, D], F32, a2)
    a3 = [arena0]
    histf = sb("histf", [32, D], F32, a3)
    histb = sb("histb", [32, D], BF16, a3)
    ckf = sb("ckf", [128, 128], F32, a3)
    ckb = sb("ckb", [128, 128], BF16, a3)
    cvf = sb("cvf", [128, 128], F32, a3)

    tp = nc.alloc_psum_tensor("tp", [128, KC, 128], BF16)
    pbig = nc.alloc_psum_tensor("pbig", [128, 3584], F32)

    def psk(lo, hi):
        return ["ps%d" % i for i in range(lo // 512, (hi + 511) // 512)]

    def pjc(col, w):
        return pbig[:, col:col + w], psk(col, col + w)

    def pjvo(s, lo, hi):
        sel, part = s // 3, s % 3
        return pjc(1024 * sel + 256 * part + lo, hi - lo)

    def pjv(s, w):
        sel, part = s // 3, s % 3
        return pjc(1024 * sel + 256 * part, w)

    BIN, BDW, LNG, LNB, G0, GKV, G1, SNK = 0, 24, 32, 40, 48, 56, 64, 72
    GA, BGLH, GQ, ESK = 0, 8, 16, 24

    scope_stack = []

    def scope(name):
        if not PROFILE:
            return
        if scope_stack:
            nm, sid = scope_stack.pop()
            nc.leave_named_scope(nm, sid, False)
        if name is not None:
            sid, _ = nc.enter_named_scope(name, False)
            scope_stack.append((name, sid))

    scope("setup_const")
    P.dma("sp", "dmisc", sp_[:], smallp[:, :], writes=["smallp"])
    P.dma("sp", "dmisc", flag[:], flag_d[:, :], writes=["flag"])
    P.dma("sp", "dmisc", wst[:], wdw_d[:, :], writes=["wst"])
    P.dma("sp", "dmisc", rowst[:], rows_d.rearrange("(o r) d -> o r d", o=1), writes=["rowst"])
    P.dma("sp", "dmisc", pg[0][:], pg_d[0], writes=["pg0"])
    P.dma("sp", "dmisc", pg[1][:], pg_d[1], writes=["pg1"])
    P.op("dve", lambda e: e.tensor_single_scalar(spd[:, GA:GA + 8], sp_[:, G0:G0 + 8], 0.5, ALU.mult),
         reads=["smallp"], writes=["spd_a"])
    P.op("dve", lambda e: e.tensor_single_scalar(spd[:, BGLH:BGLH + 8], sp_[:, BIN + 8:BIN + 16], 0.5, ALU.mult),
         reads=["smallp"], writes=["spd_b"])
    P.op("dve", lambda e: e.tensor_single_scalar(spd[:, GQ:GQ + 8], sp_[:, G1:G1 + 8], 0.125, ALU.mult),
         reads=["smallp"], writes=["spd_c"])
    P.op("act", lambda e: e.activation(spd[:, ESK:ESK + 8], sp_[:, SNK:SNK + 8], AF.Exp), reads=["smallp"], writes=["spd_d"])
    P.op("dve", lambda e: e.tensor_single_scalar(barow[:], rowst[:, 0, :], 0.5, ALU.mult), reads=["rowst"], writes=["barow"])
    P.op("dve", lambda e: e.tensor_copy(borow[:], rowst[:, 1, :]), reads=["rowst"], writes=["borow"])

    P.op("pool", lambda e: e.iota(idi[:], [[1, 128]], base=0, channel_multiplier=-1), writes=["idi"])
    P.op("dve", lambda e: e.tensor_copy(dqk[:], idi[:]), reads=["idi"], writes=["dqk"])
    P.op("dve", lambda e: e.tensor_single_scalar(idf[:], dqk[:], 0.0, ALU.is_equal), reads=["dqk"], writes=["idf"])
    P.op("dve", lambda e: e.tensor_copy(idb[:], idf[:]), reads=["idf"], writes=["idb"])
    P.op("dve", lambda e: e.memset(ones[:], 1.0), writes=["ones"])
    P.op("dve", lambda e: e.memset(mhalf[:], -0.5), writes=["mhalf"])
    P.op("dve", lambda e: e.memset(epsc[:], EPS), writes=["epsc"])
    P.op("pool", lambda e: e.memset(Uc[:], 0.0), writes=["Uc%d" % c for c in range(KC)])
    P.op("pool", lambda e: e.memset(kT[:], 0.0), writes=["kT"])
    P.op("pool", lambda e: e.memset(Vx[:], 0.0), writes=["Vx0", "Vx1", "Vx2"])
    P.op("pool", lambda e: e.memset(stat[:], 0.0), writes=["st%d" % i for i in range(64)])
    P.op("dve", lambda e: e.tensor_single_scalar(m32[:], dqk[:, 0:32], 0.0, ALU.is_equal), reads=["dqk"], writes=["m32"])
    for g in range(1, 4):
        P.op("dve", lambda e, g=g: e.tensor_single_scalar(m32t[:], dqk[:, 0:32], -32.0 * g, ALU.is_equal),
             reads=["dqk"], writes=["m32t"])
        P.op("dve", lambda e: e.tensor_tensor(m32[:], m32[:], m32t[:], ALU.add), reads=["m32", "m32t"], writes=["m32"])
    P.op("dve", lambda e: e.tensor_tensor(Wc[:], wst[:].unsqueeze(2).to_broadcast([128, 256, 32]),
                                          m32[:].unsqueeze(1).to_broadcast([128, 256, 32]), ALU.mult),
         reads=["wst", "m32"], writes=["Wc"])
    P.op("dve", lambda e: e.tensor_single_scalar(dA[:], dqk[:], 128.0, ALU.add), reads=["dqk"], writes=["dA"])
    P.op("dve", lambda e: e.tensor_single_scalar(dB[:], dqk[:], -1.0, ALU.mult), reads=["dqk"], writes=["dB"])
    P.op("dve", lambda e: e.tensor_tensor(dB[:], dB[:], dqk[:], ALU.max), reads=["dB", "dqk"], writes=["dB"])
    P.op("dve", lambda e: e.memset(mkA[:], 0.0), writes=["mkA"])
    P.op("dve", lambda e: e.memset(mkB[:], 0.0), writes=["mkB"])
    P.op("dve", lambda e: e.memset(mkA[0:64, 64:128], NEG), reads=["mkA"], writes=["mkA"])
    P.op("dve", lambda e: e.memset(mkB[64:128, 0:64], NEG), reads=["mkB"], writes=["mkB"])
    for hh in range(16):
        sl = -(2.0 ** (-(hh + 1) / 2.0))
        P.op("dve", lambda e, hh=hh, sl=sl: e.scalar_tensor_tensor(biasT[0][:, hh, :], dA[:], sl, mkA[:], ALU.mult, ALU.add),
             reads=["dA", "mkA"], writes=["biasA"])
        P.op("dve", lambda e, hh=hh, sl=sl: e.scalar_tensor_tensor(biasT[1][:, hh, :], dB[:], sl, mkB[:], ALU.mult, ALU.add),
             reads=["dB", "mkB"], writes=["biasB"])
    scope("setup_w")
    piece_i = [0]
    scr_i = [0]

    def wpiece(src_rows, ncols, gscal, mul, dst=None, dstkey=None, scr=None):
        i = piece_i[0] % 6
        q = ("sp", "pool", "act")[piece_i[0] % 3]
        piece_i[0] += 1
        P.dma(q, "dw%d%s" % (i, q), stg[i][:, 0:ncols], src_rows, writes=["stg%d" % i])
        if dst is not None:
            tgt, tkey = dst, dstkey
        else:
            j = scr_i[0] % 2
            scr_i[0] += 1
            tgt, tkey = bst[j][:, 0:ncols], "bst%d" % j
        eng = "dve" if piece_i[0] % 2 == 0 else "act"
        rk = ["stg%d" % i, "smallp", "spd_a", "spd_c"]
        if eng == "dve":
            if gscal is None:
                P.op("dve", lambda e: e.tensor_copy(tgt, stg[i][:, 0:ncols]), reads=rk, writes=[tkey])
            else:
                P.op("dve", lambda e: e.tensor_single_scalar(tgt, stg[i][:, 0:ncols], gscal, ALU.mult), reads=rk, writes=[tkey])
        else:
            if gscal is None:
                P.op("act", lambda e: e.activation(tgt, stg[i][:, 0:ncols], AF.Identity), reads=rk, writes=[tkey])
            else:
                P.op("act", lambda e: e.activation(tgt, stg[i][:, 0:ncols], AF.Identity, scale=gscal), reads=rk, writes=[tkey])
        if scr is not None:
            p0, kc = scr
            P.dma("sp", "dscr", wscr[p0:p0 + 4, :, kc * 256:(kc + 1) * 256].rearrange("q p n -> p q n"),
                  bst[j][:, 0:ncols].rearrange("p (q n) -> p q n", q=4), reads=["bst%d" % j], writes=["wscr"])

    for kc in range(KC):
        rows = slice(kc * 128, (kc + 1) * 128)
        for cb_ in range(3):
            gs = spd[:, GA + kc:GA + kc + 1] if cb_ == 0 else sp_[:, G0 + kc:G0 + kc + 1]
            wpiece(w_in0[rows, cb_ * 1024:(cb_ + 1) * 1024], 1024, gs, None,
                   dst=Win0[:, kc, cb_ * 1024:(cb_ + 1) * 1024], dstkey="Win0")
    for kc in range(KC):
        rows = slice(kc * 128, (kc + 1) * 128)
        wpiece(w_out0[rows, :], 1024, None, None, dst=Wout0[:, kc, :], dstkey="Wout0")
        wpiece(w_kv[rows, :], 256, sp_[:, GKV + kc:GKV + kc + 1], None, dst=Wkv[:, kc, :], dstkey="Wkv")
    bg_list = []
    for kc in range(KC):
        rows = slice(kc * 128, (kc + 1) * 128)
        bg_list.append((w_in1[rows, 0:1024], spd[:, GQ + kc:GQ + kc + 1], (0, kc)))
        bg_list.append((w_in1[rows, 1024:2048], sp_[:, G1 + kc:G1 + kc + 1], (4, kc)))
        bg_list.append((w_out1[rows, :], None, (8, kc)))
    bg_state = {"dma": 0, "cast": 0}

    def bg_dma():
        j = bg_state["dma"]
        if j >= len(bg_list):
            return
        bg_state["dma"] += 1
        src_rows, _, _ = bg_list[j]
        i = j % 3
        q = ("sp", "pool", "act")[j % 3]
        P.dma(q, "bg%d%s" % (i, q), ringf[i][:], src_rows, writes=["ring%d" % i])

    def bg_pump(n=1):
        for _ in range(n):
            j = bg_state["cast"]
            if j >= len(bg_list):
                return
            while bg_state["dma"] < min(j + 3, len(bg_list)):
                bg_dma()
            bg_state["cast"] += 1
            _, gscal, (p0, kc) = bg_list[j]
            i = j % 3
            jb = j % 2
            eng = "dve" if j % 2 == 0 else "act"
            rk = ["ring%d" % i, "smallp", "spd_c"]
            wk = ["ring3", "bstr%d" % jb]
            if eng == "dve":
                if gscal is None:
                    P.op("dve", lambda e: e.tensor_copy(bstr[jb][:], ringf[i][:]), reads=rk, writes=wk)
                else:
                    P.op("dve", lambda e: e.tensor_single_scalar(bstr[jb][:], ringf[i][:], gscal, ALU.mult), reads=rk, writes=wk)
            else:
                if gscal is None:
                    P.op("act", lambda e: e.activation(bstr[jb][:], ringf[i][:], AF.Identity), reads=rk, writes=wk)
                else:
                    P.op("act", lambda e: e.activation(bstr[jb][:], ringf[i][:], AF.Identity, scale=gscal), reads=rk, writes=wk)
            P.dma("sp", "dscr%d" % jb, wscr[p0:p0 + 4, :, kc * 256:(kc + 1) * 256].rearrange("q p n -> p q n"),
                  bstr[jb][:].rearrange("p (q n) -> p q n", q=4), reads=["bstr%d" % jb], writes=["wscr%d" % jb])
            if bg_state["dma"] < len(bg_list):
                bg_dma()

    P.barrier()
    stopped = DEBUG_STOP == 'setup'

    stc = [0]

    def newstat():
        c = stc[0] % 64
        stc[0] += 1
        return c, "st%d" % c

    ring_seq = []
    ring_issued = [0]

    def ring_ensure(upto):
        while ring_issued[0] < min(upto + 1, len(ring_seq)):
            gi = ring_issued[0]
            pcs = ring_seq[gi]
            sl = gi % 4
            P.dma("sp", "dring%d" % sl, ring[sl][:], wscr[pcs].rearrange("p (kc n) -> p kc n", kc=KC),
                  reads=["wscr", "wscr0", "wscr1"], writes=["ring%d" % sl])
            ring_issued[0] += 1

    def rstd_from(cols_keys):
        (c0_, k0_) = cols_keys[0]
        cm, km = newstat()
        P.op("pool", lambda e: e.tensor_tensor(stat[:, cm:cm + 1], stat[:, c0_:c0_ + 1], epsc[:, 0:1], ALU.add),
             reads=[k0_, "epsc"], writes=[km])
        cr, kr = newstat()
        P.op("pool", lambda e: e.tensor_tensor(stat[:, cr:cr + 1], stat[:, cm:cm + 1], mhalf[:, 0:1], ALU.pow),
             reads=[km, "mhalf"], writes=[kr])
        return cr, kr

    hbi = [0]

    def norm_sq(xslot):
        xk = "xs%d" % xslot
        c, k = newstat()
        P.op("act", lambda e: e.activation(junk[:], xs[xslot][:], AF.Square, scale=1.0 / 32.0, accum_out=stat[:, c:c + 1]),
             reads=[xk], writes=["junk", k])
        return rstd_from([(c, k)])

    def norm_scale(xslot, rs):
        cr, kr = rs
        i = hbi[0] % 2
        hbi[0] += 1
        P.op("act", lambda e: e.activation(hb[i][:], xs[xslot][:], AF.Identity, scale=stat[:, cr:cr + 1]),
             reads=["xs%d" % xslot, kr], writes=["hb%d" % i])
        return i

    def norm_front(xslot):
        return norm_scale(xslot, norm_sq(xslot))

    def norm_back(i, dstT, dkey, b):
        P.pe([lambda e, kc=kc: e.transpose(tp[:, kc, :], hb[i][:, kc * 128:(kc + 1) * 128], idb[:]) for kc in range(KC)],
             reads=["hb%d" % i, "idb"], writes=["tp"])
        P.op("dve", lambda e: e.tensor_copy(dstT[:, :, b * 128:(b + 1) * 128], tp[:]), reads=["tp"], writes=[dkey])

    def norm_transpose(xslot, dstT, dkey, b):
        norm_back(norm_front(xslot), dstT, dkey, b)

    def post_norm_residual(xslot, pgi, ycol):
        c, k = newstat()
        P.op("act", lambda e: e.activation(junk[:, 0:1024], pbig[:, ycol:ycol + 1024], AF.Square, scale=1.0 / 32.0, accum_out=stat[:, c:c + 1]),
             reads=psk(ycol, ycol + 1024), writes=["junk", k])
        cks = [(c, k)]
        cr, kr = rstd_from(cks)
        for nh in range(2):
            lo = ycol + nh * 512
            P.op("dve", lambda e, nh=nh, lo=lo: e.scalar_tensor_tensor(pbig[:, lo:lo + 512], pbig[:, lo:lo + 512],
                                                                      stat[:, cr:cr + 1], pg[pgi][:, nh * 512:(nh + 1) * 512],
                                                                      ALU.mult, ALU.mult),
                 reads=psk(lo, lo + 512) + [kr, "pg%d" % pgi], writes=psk(lo, lo + 512))
            P.op("dve", lambda e, nh=nh, lo=lo: e.tensor_tensor(xs[xslot][:, nh * 512:(nh + 1) * 512],
                                                               xs[xslot][:, nh * 512:(nh + 1) * 512], pbig[:, lo:lo + 512],
                                                               ALU.add),
                 reads=psk(lo, lo + 512) + ["xs%d" % xslot], writes=["xs%d" % xslot])

    tiles = [dict(kind="halo", row0=0, W=256, l1=False, fin=None)]
    for t in range(NPT):
        tiles.append(dict(kind="prompt", row0=256 + 256 * t, W=256, l1=True, fin=(0 if t == NPT - 1 else None),
                          yrow=256 * t))
    tiles.append(dict(kind="sample", row0=2304, W=128, l1=True, fin=1, yrow=2048))
    for tl in tiles:
        if tl["l1"]:
            tl["ring0"] = len(ring_seq)
            ring_seq.extend(list(range(8)) + [8, 9, 10, 11] * (tl["W"] // 128))

    xcnt = [0]
    xslot_of = {}

    def issue_x(ti):
        tl = tiles[ti]
        for b in range(tl["W"] // 128):
            sl = xcnt[0] % 4
            xcnt[0] += 1
            xslot_of[(ti, b)] = sl
            P.dma("sp", "dx%d" % sl, xs[sl][:], xin[tl["row0"] + b * 128: tl["row0"] + (b + 1) * 128, :],
                  writes=["xs%d" % sl])

    u4c = [0]

    s1_done = set()

    def stage1(ti):
        s1_done.add(ti)
        for b in range(tiles[ti]["W"] // 128):
            norm_transpose(xslot_of[(ti, b)], hT, "hT", b)

    s1_pending = {}

    def stage1_front(ti):
        s1_done.add(ti)
        s1_pending[ti] = [norm_front(xslot_of[(ti, b)]) for b in range(tiles[ti]["W"] // 128)]

    def stage1_back(ti):
        for b, i in enumerate(s1_pending.pop(ti)):
            norm_back(i, hT, "hT", b)

    def run_tiles():
      issue_x(0)
      for ti, tl in enumerate(tiles):
        W = tl["W"]
        nb = W // 128
        halo = tl["kind"] == "halo"
        c0 = 128 if halo else 0
        c0a = 96 if halo else 0
        blocks = [1] if halo else list(range(W // 128))
        fin = tl["fin"]
        if ti + 1 < len(tiles):
            issue_x(ti + 1)
        if tl["kind"] == "sample":
            P.dma("sp", "dmisc", histf[:], hist[:, :], writes=["histf"])
            P.dma("sp", "dmisc", ckf[:], ck[:, :], writes=["ckf"])
            P.dma("sp", "dmisc", cvf[:], cv[:, :], writes=["cvf"])
            P.op("dve", lambda e: e.tensor_copy(histb[:], histf[:]), reads=["histf"], writes=["histb"])
            P.op("dve", lambda e: e.tensor_copy(ckb[:], ckf[:]), reads=["ckf"], writes=["ckb"])
            P.pe([lambda e, kc=kc: e.transpose(tp[:, kc, 0:32], histb[:, kc * 128:(kc + 1) * 128], idb[0:32, 0:32])
                  for kc in range(KC)], reads=["histb", "idb"], writes=["tp"])
            P.op("dve", lambda e: e.tensor_copy(Uc[:, :, 0:32], tp[:, :, 0:32]), reads=["tp"],
                 writes=["Uc%d" % c for c in range(KC)])
            P.pe([lambda e: e.transpose(tp[:, 0, :], ckb[:], idb[:])], reads=["ckb", "idb"], writes=["tp"])
            P.op("dve", lambda e: e.tensor_copy(kT[:, 0:128], tp[:, 0, :]), reads=["tp"], writes=["kT"])
            P.op("dve", lambda e: e.tensor_copy(Vx[:, 0, :, 0:64], cvf[:].rearrange("p (k d) -> p k d", k=2)),
                 reads=["cvf"], writes=["Vx0"])
            P.op("dve", lambda e: e.memset(Vx[:, 0, :, 64:128], 1.0), reads=["Vx0"], writes=["Vx0"])
            P.dma("sp", "dout", kvo[2, 0:64, :], ck[64:128, :])
            P.dma("sp", "dout", kvo[3, 0:64, :], cv[64:128, :])
            P.barrier()

        scope("T%d_s1" % ti)
        if ti not in s1_done:
            stage1(ti)

        _chk('t%d_s1' % ti)
        scope("T%d_s2" % ti)
        def inproj(c, parts):
            sel = c % 2
            for part in parts:
                cx = c0 if part == 2 else c0a
                pv, pk = pjvo(3 * sel + part, cx, W)
                col0 = part * D + c * 128
                fns = [lambda e, kc=kc, pv=pv, col0=col0, cx=cx: e.matmul(pv, Win0[:, kc, col0:col0 + 128], hT[:, kc, cx:W],
                                                                   start=(kc == 0), stop=(part != 0 and kc == KC - 1))
                       for kc in range(KC)]
                if part == 0:
                    fns.append(lambda e, pv=pv, cx=cx: e.matmul(pv, barow[0:1, c * 128:(c + 1) * 128], ones[0:1, cx:W],
                                                         start=False, stop=True))
                P.pe(fns, reads=["hT", "Win0", "barow", "ones"], writes=pk)

        def evac_u(c):
            sel = c % 2
            pa, pak = pjvo(3 * sel + 0, c0a, W)
            pgl, pglk = pjvo(3 * sel + 1, c0a, W)
            i = c % 2
            P.op("act", lambda e: e.activation(th[i][:, c0a:W], pgl, AF.Tanh, bias=spd[:, BGLH + c:BGLH + c + 1], scale=0.5),
                 reads=pglk + ["spd_b"], writes=["th%d" % i])
            P.op("dve", lambda e: e.scalar_tensor_tensor(Uc[:, c, 32 + c0a:32 + W], th[i][:, c0a:W], 1.0, pa, ALU.add, ALU.mult),
                 reads=pak + ["th%d" % i], writes=["Uc%d" % c])
            if fin is not None:
                lo, hi = (W - 32, W) if fin == 0 else (32, 64)
                P.op("dve", lambda e: e.scalar_tensor_tensor(ufin[:, c, :], th[i][:, lo:hi], 1.0,
                                                             pbig[:, 1024 * sel + lo:1024 * sel + hi], ALU.add, ALU.mult),
                     reads=pak + ["th%d" % i], writes=["ufin"])

        def evac_gate(c):
            sel = c % 2
            pgt, pgtk = pjvo(3 * sel + 2, c0, W)
            P.op("act", lambda e: e.activation(sgate[:, c, c0:W], pgt, AF.Silu, bias=sp_[:, BIN + 16 + c:BIN + 17 + c]),
                 reads=pgtk + ["smallp"], writes=["sgate%d" % c])

        def conv(c):
            sel = c % 2
            order = [(fb, m) for fb in range(4) for m in range(8)]
            if CONV_INTERLEAVE:
                order = [(fb, m) for m in range(8) for fb in range(4)]
            fns = [lambda e, fb=fb, m=m: e.matmul(pbig[32 * fb:32 * fb + 32, 1024 * sel + 768 + c0:1024 * sel + 768 + W],
                                                  Wc[:, (c * 4 + fb) * 8 + m, :], U4[:, c, fb, 2 + m + c0:2 + m + W],
                                                  start=(m == 0), stop=(m == 7), tile_position=(0, 32 * fb))
                   for (fb, m) in order]
            pk = psk(1024 * sel + 768, 1024 * sel + 768 + W)
            P.pe(fns, reads=["U4_%d_%d_%d" % (c // 4, g, fb) for g in range(4) for fb in range(4)] + ["Wc"], writes=pk)
            pcv = pbig[:, 1024 * sel + 768 + c0:1024 * sel + 768 + W]
            P.op("act", lambda e: e.activation(cb[:, c, c0:W], pcv, AF.Identity, bias=sp_[:, BDW + c:BDW + c + 1]),
                 reads=pk + ["smallp"], writes=["cb%d" % c])
            P.op("act", lambda e: e.activation(csq[:, c, c0:W], pcv, AF.Square, bias=sp_[:, BDW + c:BDW + c + 1]),
                 reads=pk + ["smallp"], writes=["csq%d" % c])

        def stats_all():
            P.pe([lambda e, c=c: e.matmul(pbig[:, 2048 + c0:2048 + W], ones[:, 0:128], cb[:, c, c0:W], start=(c == 0),
                                          stop=(c == KC - 1)) for c in range(KC)],
                 reads=["cb%d" % c for c in range(KC)] + ["ones"], writes=psk(2048, 2560))
            P.pe([lambda e, c=c: e.matmul(pbig[:, 2304 + c0:2304 + W], ones[:, 0:128], csq[:, c, c0:W], start=(c == 0),
                                          stop=(c == KC - 1)) for c in range(KC)],
                 reads=["csq%d" % c for c in range(KC)] + ["ones"], writes=psk(2048, 2560))

        s2step = [0]

        def s2_pump():
            s2step[0] += 1
            if s2step[0] % (2 if ti == 0 else 4) == 0:
                bg_pump(1)

        L = W + 32
        qn = [0]

        def shift_copies(h):
            for g in range(4):
                for fb in range(4):
                    q = ("sp", "du4%d" % (2 * h)) if qn[0] % 2 == 0 else ("pool", "du4%d" % (2 * h + 1))
                    qn[0] += 1
                    P.dma(q[0], q[1], U4[32 * g:32 * g + 32, 4 * h:4 * h + 4, fb, 0:L],
                          Uc[32 * fb:32 * fb + 32, 4 * h:4 * h + 4, 8 * g:8 * g + L],
                          reads=["Uc%d" % c for c in range(4 * h, 4 * h + 4)], writes=["U4_%d_%d_%d" % (h, g, fb)])

        inproj(0, (0, 1))
        for c in range(KC):
            if c + 1 < KC:
                inproj(c + 1, (0, 1))
            evac_u(c)
            s2_pump()
            if c == 3:
                shift_copies(0)
        shift_copies(1)
        for c in range(KC):
            inproj(c, (2,))
            evac_gate(c)
            s2_pump()
        for c in range(KC):
            conv(c)
            s2_pump()
        stats_all()
        uck = ["Uc%d" % c for c in range(KC)]
        if halo:
            P.op("dve", lambda e: e.tensor_single_scalar(Uc[:, :, 2:32], Uc[:, :, W + 2:W + 32], flag[:, 0:1], ALU.mult),
                 reads=uck + ["flag"], writes=uck)
        else:
            P.op("dve", lambda e: e.tensor_copy(Uc[:, :, 2:32], Uc[:, :, W + 2:W + 32]), reads=uck, writes=uck)
        if fin is not None:
            P.pe([lambda e, c=c: e.transpose(pbig[0:32, 2560 + c * 128:2560 + (c + 1) * 128], ufin[:, c, :], idf[:])
                  for c in range(KC)], reads=["ufin", "idf"], writes=psk(2560, 3584))
            P.op("dve", lambda e: e.tensor_copy(ytmp[0:32, :], pbig[0:32, 2560:3584]), reads=psk(2560, 3584), writes=["ytmp"])
            P.dma("sp", "dout", convo[fin], ytmp[0:32, :], reads=["ytmp"])

        _chk('t%d_s2' % ti)
        scope("T%d_s3" % ti)
        st1 = pbig[:, 2048 + c0:2048 + W]
        st2 = pbig[:, 2304 + c0:2304 + W]
        stk = psk(2048, 2560)
        P.op("dve", lambda e: e.tensor_single_scalar(lnm[:, c0:W], st1, 1.0 / D, ALU.mult), reads=stk, writes=["lnm"])
        P.op("dve", lambda e: e.tensor_tensor(lnv[:, c0:W], lnm[:, c0:W], lnm[:, c0:W], ALU.mult), reads=["lnm"], writes=["lnv"])
        P.op("dve", lambda e: e.scalar_tensor_tensor(lnA[:, c0:W], st2, 1.0 / D, lnv[:, c0:W], ALU.mult, ALU.subtract),
             reads=stk + ["lnv"], writes=["lnA"])
        P.op("act", lambda e: e.activation(lnv[:, c0:W], lnA[:, c0:W], AF.Ln, bias=epsc[:, 0:1]), reads=["lnA", "epsc"],
             writes=["lnv"])
        P.op("act", lambda e: e.activation(lnA[:, c0:W], lnv[:, c0:W], AF.Exp, scale=-0.5), reads=["lnv"], writes=["lnA"])
        P.op("dve", lambda e: e.scalar_tensor_tensor(lnB[:, c0:W], lnm[:, c0:W], -1.0, lnA[:, c0:W], ALU.mult, ALU.mult),
             reads=["lnm", "lnA"], writes=["lnB"])
        def ln_front(c):
            i = c % 4
            P.op("pool", lambda e: e.tensor_tensor(lnt4[i][:, c0:W], cb[:, c, c0:W], lnA[:, c0:W], ALU.mult),
                 reads=["cb%d" % c, "lnA"], writes=["lntq%d" % i])
            P.op("dve", lambda e: e.tensor_tensor(lnt4[i][:, c0:W], lnt4[i][:, c0:W], lnB[:, c0:W], ALU.add),
                 reads=["lntq%d" % i, "lnB"], writes=["lntq%d" % i])
            P.op("act", lambda e: e.activation(lns4[i][:, c0:W], lnt4[i][:, c0:W], AF.Silu,
                                               bias=sp_[:, LNB + c:LNB + c + 1], scale=sp_[:, LNG + c:LNG + c + 1]),
                 reads=["lntq%d" % i, "smallp"], writes=["lnsq%d" % i] + (["th0"] if i >= 2 else []))

        def ln_back(c):
            i = c % 4
            P.op("dve", lambda e: e.tensor_tensor(cs[:, c, c0:W], lns4[i][:, c0:W], sgate[:, c, c0:W], ALU.mult),
                 reads=["lnsq%d" % i, "sgate%d" % c, "hT"] + (["th0"] if i >= 2 else []), writes=["cs%d" % c])
            P.pe([lambda e, b=b, nh=nh: e.matmul(pbig[:, 1024 * b + nh * 512:1024 * b + (nh + 1) * 512],
                                                 cs[:, c, b * 128:(b + 1) * 128], Wout0[:, c, nh * 512:(nh + 1) * 512],
                                                 start=(c == 0), stop=False) for b in blocks for nh in range(2)],
                 reads=["cs%d" % c, "Wout0"], writes=psk(0, 1024 * nb))

        LA = 3
        for c in range(LA):
            ln_front(c)
        for c in range(KC):
            if c + LA < KC:
                ln_front(c + LA)
            ln_back(c)
            if c % 2 == 1:
                bg_pump(1)
        P.pe([lambda e, b=b, nh=nh: e.matmul(pbig[:, 1024 * b + nh * 512:1024 * b + (nh + 1) * 512], ones[0:1, 0:128],
                                             borow[0:1, nh * 512:(nh + 1) * 512], start=False, stop=True)
              for b in blocks for nh in range(2)], reads=["borow", "ones"], writes=psk(0, 1024 * nb))

        _chk('t%d_s3' % ti)
        scope("T%d_s45" % ti)
        for b in blocks:
            post_norm_residual(xslot_of[(ti, b)], 0, 1024 * b)
        rs_ = {b: norm_sq(xslot_of[(ti, b)]) for b in blocks}
        hi_ = {b: norm_scale(xslot_of[(ti, b)], rs_[b]) for b in blocks}
        for b in blocks:
            norm_back(hi_[b], n1T, "n1T", b)
        pkv, pkk = pjc(c0, W - c0)
        P.pe([lambda e, kc=kc: e.matmul(pkv, Wkv[:, kc, 0:128], n1T[:, kc, c0:W], start=(kc == 0), stop=(kc == KC - 1))
              for kc in range(KC)], reads=["n1T", "Wkv"], writes=pkk)
        P.op("act", lambda e: e.activation(kT[:, 128 + c0:128 + W], pkv, AF.Identity), reads=pkk, writes=["kT"])
        for b in blocks:
            pvv, pvk = pjc(1024 + 512 * b, 128)
            P.pe([lambda e, kc=kc: e.matmul(pvv, n1T[:, kc, b * 128:(b + 1) * 128], Wkv[:, kc, 128:256],
                                            start=(kc == 0), stop=(kc == KC - 1)) for kc in range(KC)],
                 reads=["n1T", "Wkv"], writes=pvk)
            vk = "Vx%d" % (1 + b)
            pv3 = pvv.rearrange("p (k d) -> p k d", k=2)
            if halo:
                P.op("dve", lambda e: e.tensor_single_scalar(Vx[:, 1 + b, :, 0:64], pv3, flag[:, 0:1], ALU.mult),
                     reads=pvk + ["flag"], writes=[vk])
                P.op("dve", lambda e: e.tensor_copy(Vx[:, 1 + b, :, 64:128],
                                                    flag[:, 0:1].unsqueeze(2).to_broadcast([128, 2, 64])),
                     reads=["flag", vk], writes=[vk])
            else:
                P.op("dve", lambda e: e.tensor_copy(Vx[:, 1 + b, :, 0:64], pv3), reads=pvk, writes=[vk])
                P.op("dve", lambda e: e.memset(Vx[:, 1 + b, :, 64:128], 1.0), reads=[vk], writes=[vk])
        _chk('t%d_s45' % ti)
        P.barrier(skip=NOARENA)

        if fin is not None:
            b = nb - 1
            pvv, pvk = pjc(1024 + 512 * b, 128)
            P.op("dve", lambda e: e.tensor_copy(vtok[:], pvv), reads=pvk, writes=["vtok"])
            pkt, pktk = pjc(512, 128)
            P.pe([lambda e, kc=kc: e.matmul(pkt, n1T[:, kc, b * 128:(b + 1) * 128], Wkv[:, kc, 0:128],
                                            start=(kc == 0), stop=(kc == KC - 1)) for kc in range(KC)],
                 reads=["n1T", "Wkv"], writes=pktk)
            P.op("dve", lambda e: e.tensor_copy(ktok[:], pkt), reads=pktk, writes=["ktok"])
            if fin == 0:
                P.dma("sp", "dout", kvo[0], ktok[:], reads=["ktok"])
                P.dma("sp", "dout", kvo[1], vtok[:], reads=["vtok"])
            else:
                P.dma("sp", "dout", kvo[2, 64:128, :], ktok[0:64, :], reads=["ktok"])
                P.dma("sp", "dout", kvo[3, 64:128, :], vtok[0:64, :], reads=["vtok"])

        if tl["l1"]:
            bg_pump(len(bg_list))
            r0 = tl["ring0"]
            scope("T%d_s6" % ti)
            if ti + 1 < len(tiles):
                stage1_front(ti + 1)
            for oc in range(16):
                gi = r0 + oc // 2
                ring_ensure(gi + 2)
                sl = gi % 4
                pv, pk = pjc(512 * (oc % 4), W)
                P.pe([lambda e, kc=kc: e.matmul(pv, ring[sl][:, kc, (oc % 2) * 128:(oc % 2) * 128 + 128], n1T[:, kc, 0:W],
                                                start=(kc == 0), stop=(kc == KC - 1)) for kc in range(KC)],
                     reads=["n1T", "ring%d" % sl], writes=pk)
                if oc < 8:
                    P.op("dve", lambda e: e.tensor_copy(qT[:, oc, 0:W], pv), reads=pk, writes=["qT"])
                else:
                    P.op("act", lambda e: e.activation(sg2[:, oc - 8, 0:W], pv, AF.Silu), reads=pk, writes=["sg2"])
            if ti + 1 < len(tiles):
                stage1_back(ti + 1)
            _chk('t%d_s6' % ti)
            scope("T%d_s78" % ti)
            def att_scores_step(b, idx):
                kv, blk = idx // 2, idx % 2
                s0 = (idx % 2) * 1024
                S3 = pbig[:, s0:s0 + 1024].rearrange("p (g q) -> p g q", g=8)
                sk = psk(s0, s0 + 1024)
                kcols = kT[64 * kv:64 * kv + 64, blk * 128 + b * 128: blk * 128 + b * 128 + 128]
                fns = []
                for hf in range(2):
                    fns.append(lambda e, hf=hf: e.matmul(S3[:, hf * 4:(hf + 1) * 4, :], idb[:],
                                                         biasT[blk][:, kv * 8 + hf * 4:kv * 8 + hf * 4 + 4, :],
                                                         start=True, stop=False))
                for g in range(8):
                    fns.append(lambda e, g=g: e.matmul(S3[:, g, :], kcols, qT[64 * kv:64 * kv + 64, g, b * 128:(b + 1) * 128],
                                                       start=False, stop=(g % 4 == 3)))
                P.pe(fns, reads=["kT", "qT", "idb", "biasA", "biasB"], writes=sk)
                for hf in range(2):
                    P.op("act", lambda e, hf=hf: e.activation(PT[b % 2][idx][:, hf * 4:(hf + 1) * 4, :],
                                                              S3[:, hf * 4:(hf + 1) * 4, :], AF.Exp),
                         reads=sk, writes=["PT%d_%d" % (b % 2, idx)])

            def att_pv_step(b, jp):
                if True:
                    pb0 = 2048 if jp % 2 == 0 else (0 if b == nb - 1 else 2560)
                    o3 = pbig[:, pb0:pb0 + 256].rearrange("p (j q) -> p j q", j=2)
                    d3 = pbig[:, pb0 + 256:pb0 + 512].rearrange("p (j q) -> p j q", j=2)
                    pok = psk(pb0, pb0 + 512)
                    fns = []
                    for jj in range(2):
                        j = 2 * jp + jj
                        for kv in range(2):
                            for (c0_, v0_) in ((pb0, 0), (pb0 + 256, 64)):
                                for blk in range(2):
                                    fns.append(lambda e, jj=jj, j=j, kv=kv, blk=blk, c0_=c0_, v0_=v0_: e.matmul(
                                        pbig[64 * kv:64 * kv + 64, c0_ + jj * 128:c0_ + (jj + 1) * 128],
                                        Vx[:, b + blk, kv, v0_:v0_ + 64], PT[b % 2][kv * 2 + blk][:, j, :],
                                        start=(blk == 0), stop=(blk == 1), tile_position=(0, 64 * kv)))
                    P.pe(fns, reads=["Vx%d" % b, "Vx%d" % (b + 1)] + ["PT%d_%d" % (b % 2, i) for i in range(4)], writes=pok)
                    rb = jp % 2
                    for jj in range(2):
                        P.op("act", lambda e, jj=jj: e.activation(dsum[:, jj, :], d3[:, jj, :], AF.Ln,
                                                                  bias=spd[:, ESK + 2 * jp + jj:ESK + 2 * jp + jj + 1]),
                             reads=pok + ["spd_d"], writes=["dsum"])
                    P.op("act", lambda e: e.activation(rdenb[rb][:], dsum[:], AF.Exp, scale=-1.0), reads=["dsum"],
                         writes=["rden%d" % rb])
                    P.op("dve", lambda e: e.tensor_tensor(onrmb[rb][:], o3, rdenb[rb][:], ALU.mult),
                         reads=pok + ["rden%d" % rb], writes=["onrm%d" % rb])
                    P.op("pool", lambda e: e.tensor_tensor(og[:, 2 * jp:2 * jp + 2, b * 128:(b + 1) * 128], onrmb[rb][:],
                                                           sg2[:, 2 * jp:2 * jp + 2, b * 128:(b + 1) * 128], ALU.mult),
                         reads=["onrm%d" % rb, "sg2"], writes=["og%d" % b])

            def out_proj1_mm(b, ycol, nqs=(0, 1, 2, 3)):
                for nq in nqs:
                    gi = r0 + 8 + 4 * b + nq
                    ring_ensure(gi + 2)
                    sl = gi % 4
                    P.pe([lambda e, kc=kc: e.matmul(pbig[:, ycol + nq * 256:ycol + (nq + 1) * 256],
                                                    og[:, kc, b * 128:(b + 1) * 128], ring[sl][:, kc, :],
                                                    start=(kc == 0), stop=(kc == KC - 1)) for kc in range(KC)],
                         reads=["og%d" % b, "ring%d" % sl], writes=psk(ycol + nq * 256, ycol + (nq + 1) * 256))

            def out_proj1_post(b, ycol):
                xsl = xslot_of[(ti, b)]
                post_norm_residual(xsl, 1, ycol)
                nrow = 128 if tl["kind"] == "prompt" else 64
                P.dma("sp", "dout", yout[tl["yrow"] + b * 128: tl["yrow"] + b * 128 + nrow, :], xs[xsl][0:nrow, :],
                      reads=["xs%d" % xsl])

            for idx in range(4):
                att_scores_step(0, idx)
            ycols = [2560, 1024]
            for b in range(nb):
                for jp in range(4):
                    att_pv_step(b, jp)
                    if b + 1 < nb:
                        att_scores_step(b + 1, jp)
                    elif nb > 1:
                        out_proj1_mm(b - 1, ycols[b - 1], (jp,))
            out_proj1_mm(nb - 1, ycols[nb - 1])
            out_proj1_post(0, ycols[0])
            for b in range(1, nb):
                out_proj1_post(b, ycols[b])
            if ti + 1 < len(tiles) and tiles[ti + 1]["l1"]:
                ring_ensure(tiles[ti + 1]["ring0"] + 2)
        _chk('t%d_s8' % ti)
        scope("T%d_end" % ti)
        if tl["kind"] != "sample":
            P.op("dve", lambda e: e.tensor_copy(kT[:, 0:128], kT[:, W:W + 128]), reads=["kT"], writes=["kT"])
            P.op("dve", lambda e: e.tensor_copy(Vx[:, 0], Vx[:, nb]), reads=["Vx%d" % nb], writes=["Vx0"])
        P.barrier(skip=NOARENA)

    if not stopped:
        try:
            run_tiles()
        except _Stop:
            pass
    scope(None)
    P.barrier()
    P.final_wait("sp", [k for k in P.sem if P.isdma[k]])
    return nc


_CACHE = {}


def kernel(**inputs):
    f32 = np.float32
    g = lambda k: np.asarray(inputs[k], dtype=f32)
    x_prompt = g("x_prompt")[0]
    x_sample = g("x_sample")
    state_conv = g("state_conv")[0]
    cache_k = g("cache_k").reshape(8, 128, 128)
    cache_v = g("cache_v").reshape(8, 128, 128)
    a_b_in = g("a_b_in")[0]
    perm = np.concatenate([np.concatenate([np.arange(j * 64, j * 64 + 64), np.arange((8 + j) * 64, (8 + j) * 64 + 64)])
                           for j in range(8)])
    w_in1 = g("b_w_in")[0]
    w_in1p = np.ascontiguousarray(np.concatenate([w_in1[:, perm], w_in1[:, 1024 + perm]], axis=1))
    w_out1p = np.ascontiguousarray(g("b_w_out")[0][perm, :])
    fm = lambda v: np.ascontiguousarray(v.reshape(-1, 128).T)
    sinks = g("b_sinks")[0]
    sinks_fm = np.concatenate([np.tile(sinks[None, 0:8], (64, 1)), np.tile(sinks[None, 8:16], (64, 1))], axis=0)
    smallp = np.zeros((128, 96), f32)
    smallp[:, 0:24] = fm(a_b_in)
    smallp[:, 24:32] = fm(g("a_b_dw")[0])
    smallp[:, 32:40] = fm(g("a_ln_g")[0])
    smallp[:, 40:48] = fm(g("a_ln_b")[0])
    smallp[:, 48:56] = fm(g("a_pre_g")[0])
    smallp[:, 56:64] = fm(g("kv_g"))
    smallp[:, 64:72] = fm(g("b_pre_g")[0])
    smallp[:, 72:80] = sinks_fm
    wdw = g("a_w_dw")[0]
    wpad = np.concatenate([wdw, np.zeros((1, 1024), f32)], axis=0).reshape(4, 8, 32, 32)
    wdw_st = np.ascontiguousarray(wpad.transpose(0, 3, 2, 1).reshape(128, 256))
    rows = np.stack([a_b_in[0:1024], g("a_b_out")[0]], axis=0)
    pgb = np.stack([np.tile(g("a_post_g")[0][None, :], (128, 1)), np.tile(g("b_post_g")[0][None, :], (128, 1))], axis=0)
    common = dict(w_in0=np.ascontiguousarray(g("a_w_in")[0]), w_out0=np.ascontiguousarray(g("a_w_out")[0]),
                  w_kv=np.ascontiguousarray(g("w_kv")), w_in1p=w_in1p, w_out1p=w_out1p, smallp=smallp,
                  wdw_st=wdw_st, rows=np.ascontiguousarray(rows), pg=np.ascontiguousarray(pgb))
    in_maps = []
    for i in range(NCORES):
        xin = np.zeros((XROWS, D), f32)
        if i > 0:
            xin[0:256] = x_prompt[2048 * i - 256:2048 * i]
        xin[256:2304] = x_prompt[2048 * i:2048 * (i + 1)]
        xin[2304:2368] = x_sample[i]
        hist = np.zeros((32, D), f32)
        hist[2:32] = state_conv[i]
        m = dict(common)
        m.update(xin=xin, hist=hist, ck=np.ascontiguousarray(cache_k[i]), cv=np.ascontiguousarray(cache_v[i]),
                 flag=np.full((128, 1), 0.0 if i == 0 else 1.0, f32))
        in_maps.append(m)
    if "nc" not in _CACHE:
        _CACHE["nc"] = build_program()
    res = run_bass_kernel_spmd(_CACHE["nc"], in_maps, core_ids=list(range(NCORES)))
    R = res.results
    y_prompt = np.concatenate([R[i]["yout"][0:2048] for i in range(NCORES)], axis=0)[None].astype(f32)
    y_sample = np.stack([R[i]["yout"][2048:2112] for i in range(NCORES)], axis=0).astype(f32)
    conv_p = R[NCORES - 1]["convo"][0][2:32][None, None].astype(f32)
    conv_s = np.stack([R[i]["convo"][1][2:32] for i in range(NCORES)], axis=0)[None].astype(f32)
    k_p = R[NCORES - 1]["kvo"][0].reshape(1, 128, 2, 64).astype(f32)
    v_p = R[NCORES - 1]["kvo"][1].reshape(1, 128, 2, 64).astype(f32)
    k_s = np.stack([R[i]["kvo"][2].reshape(128, 2, 64) for i in range(NCORES)], axis=0).astype(f32)
    v_s = np.stack([R[i]["kvo"][3].reshape(128, 2, 64) for i in range(NCORES)], axis=0).astype(f32)
    return (y_prompt, y_sample, conv_p, conv_s, k_p, v_p, k_s, v_s)
```

```python
import numpy as np
import concourse.bass as bass
import concourse.mybir as mybir
from concourse.bass_utils import run_bass_kernel_spmd

F32 = mybir.dt.float32
BF16 = mybir.dt.bfloat16
I32 = mybir.dt.int32
AF = mybir.ActivationFunctionType
ALU = mybir.AluOpType

NCORES = 8
D = 1024
KC = 8
WT = 256
EPS = 1e-6
NPT = 8
XROWS = 256 + 2048 + 128
YROWS = 2048 + 64
NEG = -30000.0
CONV_INTERLEAVE = False
DEBUG_STOP = None
NOARENA = ("d_dx", "dx", "dring", "dout", "dscr", "bg", "dw")
DEBUG_FLAGS = set()
PROFILE = False


class _Stop(Exception):
    pass


def _chk(name):
    if DEBUG_STOP is not None and name == DEBUG_STOP:
        raise _Stop()


class Prog:
    def __init__(self, nc):
        self.nc = nc
        self.h = {}
        self.sem = {}
        self.cnt = {}
        self.isdma = {}
        self.waited = {}
        self.lw = {}
        self.rd = {}

    def eng(self, name, handle, compute=True):
        self.h[name] = handle
        if compute:
            self.sem[name] = self.nc.alloc_semaphore("s_" + name)
            self.cnt[name] = 0
            self.isdma[name] = False

    def dsem(self, name):
        self.sem[name] = self.nc.alloc_semaphore("d_" + name)
        self.cnt[name] = 0
        self.isdma[name] = True

    def _wait(self, e, reads, writes):
        deps = {}
        for k in list(reads) + list(writes):
            if k in self.lw:
                p, n = self.lw[k]
                deps[p] = max(deps.get(p, 0), n)
        for k in writes:
            for p, n in self.rd.get(k, {}).items():
                deps[p] = max(deps.get(p, 0), n)
        for p, n in deps.items():
            if self.isdma[p]:
                if self.waited.get((e, p), 0) >= n:
                    continue
                n = self.cnt[p]
            if p == e and e == "pe":
                continue
            if self.waited.get((e, p), 0) < n:
                self.h[e].wait_ge(self.sem[p], n)
                self.waited[(e, p)] = n

    def _mark(self, prod, n, reads, writes):
        for k in writes:
            self.lw[k] = (prod, n)
            self.rd[k] = {}
        for k in reads:
            self.rd.setdefault(k, {})[prod] = n

    def op(self, e, fn, reads=(), writes=()):
        self._wait(e, reads, writes)
        ins = fn(self.h[e])
        ins.then_inc(self.sem[e], 1)
        self.cnt[e] += 1
        self._mark(e, self.cnt[e], reads, writes)

    def pe(self, fns, reads=(), writes=()):
        self._wait("pe", reads, writes)
        for f in fns[:-1]:
            f(self.h["pe"])
        ins = fns[-1](self.h["pe"])
        ins.then_inc(self.sem["pe"], 1)
        self.cnt["pe"] += 1
        self._mark("pe", self.cnt["pe"], reads, writes)

    def dma(self, q, ds, out, in_, reads=(), writes=()):
        self._wait(q, reads, writes)
        self.h[q].dma_start(out=out, in_=in_).then_inc(self.sem[ds], 16)
        self.cnt[ds] += 16
        self._mark(ds, self.cnt[ds], reads, writes)

    def barrier(self, skip=()):
        for e in self.h:
            for p in self.sem:
                if p == e and e == "pe":
                    continue
                if any(p.startswith(x) for x in skip):
                    continue
                n = self.cnt[p]
                if n > 0 and self.waited.get((e, p), 0) < n:
                    self.h[e].wait_ge(self.sem[p], n)
                    self.waited[(e, p)] = n

    def final_wait(self, q, names):
        for p in names:
            n = self.cnt[p]
            if n > 0:
                self.h[q].wait_ge(self.sem[p], n)


def build_program():
    nc = bass.Bass("TRN2", target_bir_lowering=False)
    P = Prog(nc)
    P.eng("pe", nc.tensor)
    P.eng("act", nc.scalar)
    P.eng("dve", nc.vector)
    P.eng("pool", nc.gpsimd)
    P.eng("sp", nc.sync, compute=False)
    for i in range(4):
        P.dsem("dx%d" % i)
        P.dsem("dring%d" % i)
    for i in range(4):
        P.dsem("du4%d" % i)
    for nm in ["dw%d%s" % (i, q) for i in range(6) for q in ("sp", "pool", "act")] + ["bg%d%s" % (i, q) for i in range(3) for q in ("sp", "pool", "act")] + ["dscr", "dscr0", "dscr1", "dout", "dmisc"]:
        P.dsem(nm)

    def din(name, shape):
        return nc.dram_tensor(name, list(shape), F32, kind="ExternalInput").ap()

    def dout(name, shape):
        return nc.dram_tensor(name, list(shape), F32, kind="ExternalOutput").ap()

    xin = din("xin", [XROWS, D])
    hist = din("hist", [32, D])
    ck = din("ck", [128, 128])
    cv = din("cv", [128, 128])
    flag_d = din("flag", [128, 1])
    w_in0 = din("w_in0", [D, 3 * D])
    w_out0 = din("w_out0", [D, D])
    w_kv = din("w_kv", [D, 256])
    w_in1 = din("w_in1p", [D, 2 * D])
    w_out1 = din("w_out1p", [D, D])
    smallp = din("smallp", [128, 96])
    wdw_d = din("wdw_st", [128, 256])
    rows_d = din("rows", [2, D])
    pg_d = din("pg", [2, 128, D])
    yout = dout("yout", [YROWS, D])
    convo = dout("convo", [2, 32, D])
    kvo = dout("kvo", [4, 128, 128])
    wscr = nc.dram_tensor("wscr", [12, 128, KC * 256], BF16, kind="Internal").ap()

    base = [16512]
    LIMIT = 16512 + 212864

    sb_off = {}

    def sb(name, shape, dt, at=None):
        n = int(np.prod(shape[1:])) * (4 if dt in (F32, I32) else 2)
        n = (n + 31) // 32 * 32
        if at is None:
            off = base[0]
            base[0] += n
            assert base[0] <= LIMIT, (name, base[0])
        else:
            off = at[0]
            at[0] += n
            assert at[0] <= LIMIT, (name, at[0])
        sb_off[name] = off
        return nc.alloc_sbuf_tensor_at(name, list(shape), dt, offset=off)

    Win0 = sb("Win0", [128, KC, 3 * D], BF16)
    Wout0 = sb("Wout0", [128, KC, D], BF16)
    Wkv = sb("Wkv", [128, KC, 256], BF16)
    Wc = sb("Wc", [128, 256, 32], BF16)
    biasT = [sb("biasA", [128, 16, 128], BF16), sb("biasB", [128, 16, 128], BF16)]
    pg = [sb("pg0", [128, D], F32), sb("pg1", [128, D], F32)]
    barow = sb("barow", [1, D], BF16)
    borow = sb("borow", [1, D], BF16)
    idb = sb("idb", [128, 128], BF16)
    idf = sb("idf", [128, 128], F32)
    ones = sb("ones", [128, 256], BF16)
    onesD = sb("onesD", [128, 128], BF16)
    sp_ = sb("smallp", [128, 96], F32)
    spd = sb("spd", [128, 64], F32)
    kT = sb("kT", [128, 128 + WT], BF16)
    Vx = sb("Vx", [128, 3, 2, 128], BF16)
    UL = 32 + WT + 32
    Uc = sb("Uc", [128, KC, UL], BF16)
    ring = [sb("ring%d" % i, [128, KC, 256], BF16) for i in range(4)]
    ringf = [nc.alloc_sbuf_tensor_at("ringf%d" % i, [128, 1024], F32, offset=sb_off["ring%d" % i]) for i in range(3)]
    bstr = [nc.alloc_sbuf_tensor_at("bstr%d" % i, [128, 1024], BF16, offset=sb_off["ring3"] + 2048 * i) for i in range(2)]
    xs = [sb("xs%d" % i, [128, D], F32) for i in range(4)]
    hb = [sb("hb%d" % i, [128, D], BF16) for i in range(2)]
    junk = sb("junk", [128, D], BF16)
    hT = sb("hT", [128, KC, WT], BF16)
    n1T = sb("n1T", [128, KC, WT], BF16)
    stat = sb("stat", [128, 64], F32)
    mhalf = sb("mhalf", [128, WT], F32)
    flag = sb("flag", [128, 1], F32)
    epsc = sb("epsc", [128, 1], F32)
    arena0 = base[0]
    a0 = [arena0]
    cs = hT
    U4 = sb("U4", [128, KC, 4, WT + 32], BF16, a0)
    sgate = sb("sgate", [128, KC, WT], BF16, a0)
    cb = sb("cb", [128, KC, WT], BF16, a0)
    csq = sb("csq", [128, KC, WT], BF16, a0)
    th = [sb("th%d" % i, [128, WT], F32, a0) for i in range(2)]
    lnm = sb("lnm", [128, WT], F32, a0)
    lnv = sb("lnv", [128, WT], F32, a0)
    lnA = sb("lnA", [128, WT], F32, a0)
    lnB = sb("lnB", [128, WT], F32, a0)
    lnt = [[sb("lnt%d_%d" % (i, j), [128, WT], F32, a0) for j in range(2)] for i in range(2)]
    lns = [sb("lns%d" % i, [128, WT], BF16, a0) for i in range(2)]
    lnt4 = [lnt[0][0], lnt[0][1], lnt[1][0], lnt[1][1]]
    lns4 = lns + [nc.alloc_sbuf_tensor_at("lnsx%d" % i, [128, WT], BF16, offset=sb_off["th0"] + 512 * i) for i in range(2)]
    ytmp = sb("ytmp", [128, D], F32, a0)
    ufin = sb("ufin", [128, KC, 32], F32, a0)
    a1 = [arena0]
    qT = sb("qT", [128, KC, WT], BF16, a1)
    sg2 = sb("sg2", [128, KC, WT], BF16, a1)
    og = sb("og", [128, KC, WT], BF16, a1)
    PT = [[sb("PT%d_%d" % (bb, i), [128, 8, 128], BF16, a1) for i in range(4)] for bb in range(2)]
    dsum = sb("dsum", [128, 2, 128], F32, a1)
    rdenb = [sb("rden%d" % i, [128, 2, 128], F32, a1) for i in range(2)]
    onrmb = [sb("onrm%d" % i, [128, 2, 128], F32, a1) for i in range(2)]
    ktok = sb("ktok", [128, 128], F32, a1)
    vtok = sb("vtok", [128, 128], F32, a1)
    a2 = [arena0]
    stg = [sb("stg%d" % i, [128, 1024], F32, a2) for i in range(6)]
    bst = [sb("bst%d" % i, [128, 1024], BF16, a2) for i in range(2)]
    idi = sb("idi", [128, 128], I32, a2)
    dqk = sb("dqk", [128, 128], F32, a2)
    dA = sb("dA", [128, 128], F32, a2)
    dB = sb("dB", [128, 128], F32, a2)
    mkA = sb("mkA", [128, 128], F32, a2)
    mkB = sb("mkB", [128, 128], F32, a2)
    m32 = sb("m32", [128, 32], F32, a2)
    m32t = sb("m32t", [128, 32], F32, a2)
    wst = sb("wst", [128, 256], F32, a2)
    rowst = sb("rowst", [1, 2, D], F32, a2)
    a3 = [arena0]
    histf = sb("histf", [32, D], F32, a3)
    histb = sb("histb", [32, D], BF16, a3)
    ckf = sb("ckf", [128, 128], F32, a3)
    ckb = sb("ckb", [128, 128], BF16, a3)
    cvf = sb("cvf", [128, 128], F32, a3)

    tp = nc.alloc_psum_tensor("tp", [128, KC, 128], BF16)
    pbig = nc.alloc_psum_tensor("pbig", [128, 3584], F32)

    def psk(lo, hi):
        return ["ps%d" % i for i in range(lo // 512, (hi + 511) // 512)]

    def pjc(col, w):
        return pbig[:, col:col + w], psk(col, col + w)

    def pjvo(s, lo, hi):
        sel, part = s // 3, s % 3
        return pjc(1024 * sel + 256 * part + lo, hi - lo)

    def pjv(s, w):
        sel, part = s // 3, s % 3
        return pjc(1024 * sel + 256 * part, w)

    BIN, BDW, LNG, LNB, G0, GKV, G1, SNK = 0, 24, 32, 40, 48, 56, 64, 72
    GA, BGLH, GQ, ESK = 0, 8, 16, 24

    scope_stack = []

    def scope(name):
        if not PROFILE:
            return
        if scope_stack:
            nm, sid = scope_stack.pop()
            nc.leave_named_scope(nm, sid, False)
        if name is not None:
            sid, _ = nc.enter_named_scope(name, False)
            scope_stack.append((name, sid))

    scope("setup_const")
    P.dma("sp", "dmisc", sp_[:], smallp[:, :], writes=["smallp"])
    P.dma("sp", "dmisc", flag[:], flag_d[:, :], writes=["flag"])
    P.dma("sp", "dmisc", wst[:], wdw_d[:, :], writes=["wst"])
    P.dma("sp", "dmisc", rowst[:], rows_d.rearrange("(o r) d -> o r d", o=1), writes=["rowst"])
    P.dma("sp", "dmisc", pg[0][:], pg_d[0], writes=["pg0"])
    P.dma("sp", "dmisc", pg[1][:], pg_d[1], writes=["pg1"])
    P.op("dve", lambda e: e.tensor_single_scalar(spd[:, GA:GA + 8], sp_[:, G0:G0 + 8], 0.5, ALU.mult),
         reads=["smallp"], writes=["spd_a"])
    P.op("dve", lambda e: e.tensor_single_scalar(spd[:, BGLH:BGLH + 8], sp_[:, BIN + 8:BIN + 16], 0.5, ALU.mult),
         reads=["smallp"], writes=["spd_b"])
    P.op("dve", lambda e: e.tensor_single_scalar(spd[:, GQ:GQ + 8], sp_[:, G1:G1 + 8], 0.125, ALU.mult),
         reads=["smallp"], writes=["spd_c"])
    P.op("act", lambda e: e.activation(spd[:, ESK:ESK + 8], sp_[:, SNK:SNK + 8], AF.Exp), reads=["smallp"], writes=["spd_d"])
    P.op("dve", lambda e: e.tensor_single_scalar(barow[:], rowst[:, 0, :], 0.5, ALU.mult), reads=["rowst"], writes=["barow"])
    P.op("dve", lambda e: e.tensor_copy(borow[:], rowst[:, 1, :]), reads=["rowst"], writes=["borow"])

    P.op("pool", lambda e: e.iota(idi[:], [[1, 128]], base=0, channel_multiplier=-1), writes=["idi"])
    P.op("dve", lambda e: e.tensor_copy(dqk[:], idi[:]), reads=["idi"], writes=["dqk"])
    P.op("dve", lambda e: e.tensor_single_scalar(idf[:], dqk[:], 0.0, ALU.is_equal), reads=["dqk"], writes=["idf"])
    P.op("dve", lambda e: e.tensor_copy(idb[:], idf[:]), reads=["idf"], writes=["idb"])
    P.op("dve", lambda e: e.memset(ones[:], 1.0), writes=["ones"])
    P.op("dve", lambda e: e.memset(onesD[:], 1.0 / D), writes=["ones"])
    P.op("dve", lambda e: e.memset(mhalf[:], -0.5), writes=["mhalf"])
    P.op("dve", lambda e: e.memset(epsc[:], EPS), writes=["epsc"])
    P.op("pool", lambda e: e.memset(Uc[:], 0.0), writes=["Uc%d" % c for c in range(KC)])
    P.op("pool", lambda e: e.memset(kT[:], 0.0), writes=["kT"])
    P.op("pool", lambda e: e.memset(Vx[:], 0.0), writes=["Vx0", "Vx1", "Vx2"])
    P.op("pool", lambda e: e.memset(stat[:], 0.0), writes=["st%d" % i for i in range(64)])
    P.op("dve", lambda e: e.tensor_single_scalar(m32[:], dqk[:, 0:32], 0.0, ALU.is_equal), reads=["dqk"], writes=["m32"])
    for g in range(1, 4):
        P.op("dve", lambda e, g=g: e.tensor_single_scalar(m32t[:], dqk[:, 0:32], -32.0 * g, ALU.is_equal),
             reads=["dqk"], writes=["m32t"])
        P.op("dve", lambda e: e.tensor_tensor(m32[:], m32[:], m32t[:], ALU.add), reads=["m32", "m32t"], writes=["m32"])
    P.op("dve", lambda e: e.tensor_tensor(Wc[:], wst[:].unsqueeze(2).to_broadcast([128, 256, 32]),
                                          m32[:].unsqueeze(1).to_broadcast([128, 256, 32]), ALU.mult),
         reads=["wst", "m32"], writes=["Wc"])
    P.op("dve", lambda e: e.tensor_single_scalar(dA[:], dqk[:], 128.0, ALU.add), reads=["dqk"], writes=["dA"])
    P.op("dve", lambda e: e.tensor_single_scalar(dB[:], dqk[:], -1.0, ALU.mult), reads=["dqk"], writes=["dB"])
    P.op("dve", lambda e: e.tensor_tensor(dB[:], dB[:], dqk[:], ALU.max), reads=["dB", "dqk"], writes=["dB"])
    P.op("dve", lambda e: e.memset(mkA[:], 0.0), writes=["mkA"])
    P.op("dve", lambda e: e.memset(mkB[:], 0.0), writes=["mkB"])
    P.op("dve", lambda e: e.memset(mkA[0:64, 64:128], NEG), reads=["mkA"], writes=["mkA"])
    P.op("dve", lambda e: e.memset(mkB[64:128, 0:64], NEG), reads=["mkB"], writes=["mkB"])
    for hh in range(16):
        sl = -(2.0 ** (-(hh + 1) / 2.0))
        P.op("dve", lambda e, hh=hh, sl=sl: e.scalar_tensor_tensor(biasT[0][:, hh, :], dA[:], sl, mkA[:], ALU.mult, ALU.add),
             reads=["dA", "mkA"], writes=["biasA"])
        P.op("dve", lambda e, hh=hh, sl=sl: e.scalar_tensor_tensor(biasT[1][:, hh, :], dB[:], sl, mkB[:], ALU.mult, ALU.add),
             reads=["dB", "mkB"], writes=["biasB"])
    scope("setup_w")
    piece_i = [0]
    scr_i = [0]

    def wpiece(src_rows, ncols, gscal, mul, dst=None, dstkey=None, scr=None):
        i = piece_i[0] % 6
        q = ("sp", "pool", "act")[piece_i[0] % 3]
        piece_i[0] += 1
        P.dma(q, "dw%d%s" % (i, q), stg[i][:, 0:ncols], src_rows, writes=["stg%d" % i])
        if dst is not None:
            tgt, tkey = dst, dstkey
        else:
            j = scr_i[0] % 2
            scr_i[0] += 1
            tgt, tkey = bst[j][:, 0:ncols], "bst%d" % j
        eng = "dve" if piece_i[0] % 2 == 0 else "act"
        rk = ["stg%d" % i, "smallp", "spd_a", "spd_c"]
        if eng == "dve":
            if gscal is None:
                P.op("dve", lambda e: e.tensor_copy(tgt, stg[i][:, 0:ncols]), reads=rk, writes=[tkey])
            else:
                P.op("dve", lambda e: e.tensor_single_scalar(tgt, stg[i][:, 0:ncols], gscal, ALU.mult), reads=rk, writes=[tkey])
        else:
            if gscal is None:
                P.op("act", lambda e: e.activation(tgt, stg[i][:, 0:ncols], AF.Identity), reads=rk, writes=[tkey])
            else:
                P.op("act", lambda e: e.activation(tgt, stg[i][:, 0:ncols], AF.Identity, scale=gscal), reads=rk, writes=[tkey])
        if scr is not None:
            p0, kc = scr
            P.dma("sp", "dscr", wscr[p0:p0 + 4, :, kc * 256:(kc + 1) * 256].rearrange("q p n -> p q n"),
                  bst[j][:, 0:ncols].rearrange("p (q n) -> p q n", q=4), reads=["bst%d" % j], writes=["wscr"])

    for kc in range(KC):
        rows = slice(kc * 128, (kc + 1) * 128)
        for cb_ in range(3):
            gs = spd[:, GA + kc:GA + kc + 1] if cb_ == 0 else sp_[:, G0 + kc:G0 + kc + 1]
            wpiece(w_in0[rows, cb_ * 1024:(cb_ + 1) * 1024], 1024, gs, None,
                   dst=Win0[:, kc, cb_ * 1024:(cb_ + 1) * 1024], dstkey="Win0")
    for kc in range(KC):
        rows = slice(kc * 128, (kc + 1) * 128)
        wpiece(w_out0[rows, :], 1024, None, None, dst=Wout0[:, kc, :], dstkey="Wout0")
        wpiece(w_kv[rows, :], 256, sp_[:, GKV + kc:GKV + kc + 1], None, dst=Wkv[:, kc, :], dstkey="Wkv")
    bg_list = []
    for kc in range(KC):
        rows = slice(kc * 128, (kc + 1) * 128)
        bg_list.append((w_in1[rows, 0:1024], spd[:, GQ + kc:GQ + kc + 1], (0, kc)))
        bg_list.append((w_in1[rows, 1024:2048], sp_[:, G1 + kc:G1 + kc + 1], (4, kc)))
        bg_list.append((w_out1[rows, :], None, (8, kc)))
    bg_state = {"dma": 0, "cast": 0}

    def bg_dma():
        j = bg_state["dma"]
        if j >= len(bg_list):
            return
        bg_state["dma"] += 1
        src_rows, _, _ = bg_list[j]
        i = j % 3
        q = ("sp", "pool", "act")[j % 3]
        P.dma(q, "bg%d%s" % (i, q), ringf[i][:], src_rows, writes=["ring%d" % i])

    def bg_pump(n=1):
        for _ in range(n):
            j = bg_state["cast"]
            if j >= len(bg_list):
                return
            while bg_state["dma"] < min(j + 3, len(bg_list)):
                bg_dma()
            bg_state["cast"] += 1
            _, gscal, (p0, kc) = bg_list[j]
            i = j % 3
            jb = j % 2
            eng = "dve" if j % 2 == 0 else "act"
            rk = ["ring%d" % i, "smallp", "spd_c"]
            wk = ["ring3", "bstr%d" % jb]
            if eng == "dve":
                if gscal is None:
                    P.op("dve", lambda e: e.tensor_copy(bstr[jb][:], ringf[i][:]), reads=rk, writes=wk)
                else:
                    P.op("dve", lambda e: e.tensor_single_scalar(bstr[jb][:], ringf[i][:], gscal, ALU.mult), reads=rk, writes=wk)
            else:
                if gscal is None:
                    P.op("act", lambda e: e.activation(bstr[jb][:], ringf[i][:], AF.Identity), reads=rk, writes=wk)
                else:
                    P.op("act", lambda e: e.activation(bstr[jb][:], ringf[i][:], AF.Identity, scale=gscal), reads=rk, writes=wk)
            P.dma("sp", "dscr%d" % jb, wscr[p0:p0 + 4, :, kc * 256:(kc + 1) * 256].rearrange("q p n -> p q n"),
                  bstr[jb][:].rearrange("p (q n) -> p q n", q=4), reads=["bstr%d" % jb], writes=["wscr%d" % jb])
            if bg_state["dma"] < len(bg_list):
                bg_dma()

    P.barrier()
    stopped = DEBUG_STOP == 'setup'

    stc = [0]

    def newstat():
        c = stc[0] % 64
        stc[0] += 1
        return c, "st%d" % c

    ring_seq = []
    ring_issued = [0]

    def ring_ensure(upto):
        while ring_issued[0] < min(upto + 1, len(ring_seq)):
            gi = ring_issued[0]
            pcs = ring_seq[gi]
            sl = gi % 4
            P.dma("sp", "dring%d" % sl, ring[sl][:], wscr[pcs].rearrange("p (kc n) -> p kc n", kc=KC),
                  reads=["wscr", "wscr0", "wscr1"], writes=["ring%d" % sl])
            ring_issued[0] += 1

    def rstd_from(cols_keys):
        (c0_, k0_) = cols_keys[0]
        cm, km = newstat()
        P.op("pool", lambda e: e.tensor_tensor(stat[:, cm:cm + 1], stat[:, c0_:c0_ + 1], epsc[:, 0:1], ALU.add),
             reads=[k0_, "epsc"], writes=[km])
        cr, kr = newstat()
        P.op("pool", lambda e: e.tensor_tensor(stat[:, cr:cr + 1], stat[:, cm:cm + 1], mhalf[:, 0:1], ALU.pow),
             reads=[km, "mhalf"], writes=[kr])
        return cr, kr

    hbi = [0]

    def norm_sq(xslot):
        xk = "xs%d" % xslot
        c, k = newstat()
        P.op("act", lambda e: e.activation(junk[:], xs[xslot][:], AF.Square, scale=1.0 / 32.0, accum_out=stat[:, c:c + 1]),
             reads=[xk], writes=["junk", k])
        return rstd_from([(c, k)])

    def norm_scale(xslot, rs):
        cr, kr = rs
        i = hbi[0] % 2
        hbi[0] += 1
        P.op("act", lambda e: e.activation(hb[i][:], xs[xslot][:], AF.Identity, scale=stat[:, cr:cr + 1]),
             reads=["xs%d" % xslot, kr], writes=["hb%d" % i])
        return i

    def norm_front(xslot):
        return norm_scale(xslot, norm_sq(xslot))

    def norm_back(i, dstT, dkey, b):
        P.pe([lambda e, kc=kc: e.transpose(tp[:, kc, :], hb[i][:, kc * 128:(kc + 1) * 128], idb[:]) for kc in range(KC)],
             reads=["hb%d" % i, "idb"], writes=["tp"])
        P.op("dve", lambda e: e.tensor_copy(dstT[:, :, b * 128:(b + 1) * 128], tp[:]), reads=["tp"], writes=[dkey])

    def norm_transpose(xslot, dstT, dkey, b):
        norm_back(norm_front(xslot), dstT, dkey, b)

    def post_norm_residual(xslot, pgi, ycol):
        c, k = newstat()
        P.op("act", lambda e: e.activation(junk[:, 0:1024], pbig[:, ycol:ycol + 1024], AF.Square, scale=1.0 / 32.0, accum_out=stat[:, c:c + 1]),
             reads=psk(ycol, ycol + 1024), writes=["junk", k])
        cks = [(c, k)]
        cr, kr = rstd_from(cks)
        for nh in range(2):
            lo = ycol + nh * 512
            P.op("dve", lambda e, nh=nh, lo=lo: e.scalar_tensor_tensor(pbig[:, lo:lo + 512], pbig[:, lo:lo + 512],
                                                                      stat[:, cr:cr + 1], pg[pgi][:, nh * 512:(nh + 1) * 512],
                                                                      ALU.mult, ALU.mult),
                 reads=psk(lo, lo + 512) + [kr, "pg%d" % pgi], writes=psk(lo, lo + 512))
            P.op("dve", lambda e, nh=nh, lo=lo: e.tensor_tensor(xs[xslot][:, nh * 512:(nh + 1) * 512],
                                                               xs[xslot][:, nh * 512:(nh + 1) * 512], pbig[:, lo:lo + 512],
                                                               ALU.add),
                 reads=psk(lo, lo + 512) + ["xs%d" % xslot], writes=["xs%d" % xslot])

    tiles = [dict(kind="halo", row0=0, W=256, l1=False, fin=None)]
    for t in range(NPT):
        tiles.append(dict(kind="prompt", row0=256 + 256 * t, W=256, l1=True, fin=(0 if t == NPT - 1 else None),
                          yrow=256 * t))
    tiles.append(dict(kind="sample", row0=2304, W=128, l1=True, fin=1, yrow=2048))
    for tl in tiles:
        if tl["l1"]:
            tl["ring0"] = len(ring_seq)
            ring_seq.extend(list(range(8)) + [8, 9, 10, 11] * (tl["W"] // 128))

    xcnt = [0]
    xslot_of = {}

    def issue_x(ti):
        tl = tiles[ti]
        for b in range(tl["W"] // 128):
            sl = xcnt[0] % 4
            xcnt[0] += 1
            xslot_of[(ti, b)] = sl
            P.dma("sp", "dx%d" % sl, xs[sl][:], xin[tl["row0"] + b * 128: tl["row0"] + (b + 1) * 128, :],
                  writes=["xs%d" % sl])

    u4c = [0]

    s1_done = set()

    def stage1(ti):
        s1_done.add(ti)
        for b in range(tiles[ti]["W"] // 128):
            norm_transpose(xslot_of[(ti, b)], hT, "hT", b)

    s1_pending = {}

    def stage1_front(ti):
        s1_done.add(ti)
        s1_pending[ti] = [norm_front(xslot_of[(ti, b)]) for b in range(tiles[ti]["W"] // 128)]

    def stage1_back(ti):
        for b, i in enumerate(s1_pending.pop(ti)):
            norm_back(i, hT, "hT", b)

    def run_tiles():
      issue_x(0)
      for ti, tl in enumerate(tiles):
        W = tl["W"]
        nb = W // 128
        halo = tl["kind"] == "halo"
        c0 = 128 if halo else 0
        c0a = 96 if halo else 0
        blocks = [1] if halo else list(range(W // 128))
        fin = tl["fin"]
        if ti + 1 < len(tiles):
            issue_x(ti + 1)
        if tl["kind"] == "sample":
            P.dma("sp", "dmisc", histf[:], hist[:, :], writes=["histf"])
            P.dma("sp", "dmisc", ckf[:], ck[:, :], writes=["ckf"])
            P.dma("sp", "dmisc", cvf[:], cv[:, :], writes=["cvf"])
            P.op("dve", lambda e: e.tensor_copy(histb[:], histf[:]), reads=["histf"], writes=["histb"])
            P.op("dve", lambda e: e.tensor_copy(ckb[:], ckf[:]), reads=["ckf"], writes=["ckb"])
            P.pe([lambda e, kc=kc: e.transpose(tp[:, kc, 0:32], histb[:, kc * 128:(kc + 1) * 128], idb[0:32, 0:32])
                  for kc in range(KC)], reads=["histb", "idb"], writes=["tp"])
            P.op("dve", lambda e: e.tensor_copy(Uc[:, :, 0:32], tp[:, :, 0:32]), reads=["tp"],
                 writes=["Uc%d" % c for c in range(KC)])
            P.pe([lambda e: e.transpose(tp[:, 0, :], ckb[:], idb[:])], reads=["ckb", "idb"], writes=["tp"])
            P.op("dve", lambda e: e.tensor_copy(kT[:, 0:128], tp[:, 0, :]), reads=["tp"], writes=["kT"])
            P.op("dve", lambda e: e.tensor_copy(Vx[:, 0, :, 0:64], cvf[:].rearrange("p (k d) -> p k d", k=2)),
                 reads=["cvf"], writes=["Vx0"])
            P.op("dve", lambda e: e.memset(Vx[:, 0, :, 64:128], 1.0), reads=["Vx0"], writes=["Vx0"])
            P.dma("sp", "dout", kvo[2, 0:64, :], ck[64:128, :])
            P.dma("sp", "dout", kvo[3, 0:64, :], cv[64:128, :])
            P.barrier()

        scope("T%d_s1" % ti)
        if ti not in s1_done:
            stage1(ti)

        _chk('t%d_s1' % ti)
        scope("T%d_s2" % ti)
        def inproj(c, parts):
            sel = c % 2
            for part in parts:
                cx = c0 if part == 2 else c0a
                pv, pk = pjvo(3 * sel + part, cx, W)
                col0 = part * D + c * 128
                fns = [lambda e, kc=kc, pv=pv, col0=col0, cx=cx: e.matmul(pv, Win0[:, kc, col0:col0 + 128], hT[:, kc, cx:W],
                                                                   start=(kc == 0), stop=(part != 0 and kc == KC - 1))
                       for kc in range(KC)]
                if part == 0:
                    fns.append(lambda e, pv=pv, cx=cx: e.matmul(pv, barow[0:1, c * 128:(c + 1) * 128], ones[0:1, cx:W],
                                                         start=False, stop=True))
                P.pe(fns, reads=["hT", "Win0", "barow", "ones"], writes=pk)

        def evac_u(c):
            sel = c % 2
            pa, pak = pjvo(3 * sel + 0, c0a, W)
            pgl, pglk = pjvo(3 * sel + 1, c0a, W)
            i = c % 2
            P.op("act", lambda e: e.activation(th[i][:, c0a:W], pgl, AF.Tanh, bias=spd[:, BGLH + c:BGLH + c + 1], scale=0.5),
                 reads=pglk + ["spd_b"], writes=["th%d" % i])
            P.op("dve", lambda e: e.scalar_tensor_tensor(Uc[:, c, 32 + c0a:32 + W], th[i][:, c0a:W], 1.0, pa, ALU.add, ALU.mult),
                 reads=pak + ["th%d" % i], writes=["Uc%d" % c])
            if fin is not None:
                lo, hi = (W - 32, W) if fin == 0 else (32, 64)
                P.op("dve", lambda e: e.scalar_tensor_tensor(ufin[:, c, :], th[i][:, lo:hi], 1.0,
                                                             pbig[:, 1024 * sel + lo:1024 * sel + hi], ALU.add, ALU.mult),
                     reads=pak + ["th%d" % i], writes=["ufin"])

        def evac_gate(c):
            sel = c % 2
            pgt, pgtk = pjvo(3 * sel + 2, c0, W)
            P.op("act", lambda e: e.activation(sgate[:, c, c0:W], pgt, AF.Silu, bias=sp_[:, BIN + 16 + c:BIN + 17 + c]),
                 reads=pgtk + ["smallp"], writes=["sgate%d" % c])

        def conv(c):
            sel = c % 2
            order = [(fb, m) for fb in range(4) for m in range(8)]
            if CONV_INTERLEAVE:
                order = [(fb, m) for m in range(8) for fb in range(4)]
            fns = [lambda e, fb=fb, m=m: e.matmul(pbig[32 * fb:32 * fb + 32, 1024 * sel + 768 + c0:1024 * sel + 768 + W],
                                                  Wc[:, (c * 4 + fb) * 8 + m, :], U4[:, c, fb, 2 + m + c0:2 + m + W],
                                                  start=(m == 0), stop=(m == 7), tile_position=(0, 32 * fb))
                   for (fb, m) in order]
            pk = psk(1024 * sel + 768, 1024 * sel + 768 + W)
            P.pe(fns, reads=["U4_%d_%d_%d" % (c // 4, g, fb) for g in range(4) for fb in range(4)] + ["Wc"], writes=pk)
            pcv = pbig[:, 1024 * sel + 768 + c0:1024 * sel + 768 + W]
            P.op("act", lambda e: e.activation(cb[:, c, c0:W], pcv, AF.Identity, bias=sp_[:, BDW + c:BDW + c + 1]),
                 reads=pk + ["smallp"], writes=["cb%d" % c])
            P.op("act", lambda e: e.activation(csq[:, c, c0:W], pcv, AF.Square, bias=sp_[:, BDW + c:BDW + c + 1]),
                 reads=pk + ["smallp"], writes=["csq%d" % c])

        def stats_all():
            P.pe([lambda e, c=c: e.matmul(pbig[:, 2048 + c0:2048 + W], onesD[:, :], cb[:, c, c0:W], start=(c == 0),
                                          stop=(c == KC - 1)) for c in range(KC)],
                 reads=["cb%d" % c for c in range(KC)] + ["ones"], writes=psk(2048, 2560))
            P.pe([lambda e, c=c: e.matmul(pbig[:, 2304 + c0:2304 + W], onesD[:, :], csq[:, c, c0:W], start=(c == 0),
                                          stop=(c == KC - 1)) for c in range(KC)],
                 reads=["csq%d" % c for c in range(KC)] + ["ones"], writes=psk(2048, 2560))

        s2step = [0]

        def s2_pump():
            s2step[0] += 1
            if s2step[0] % (2 if ti == 0 else 4) == 0:
                bg_pump(1)

        L = W + 32
        qn = [0]

        def shift_copies(h):
            for g in range(4):
                for fb in range(4):
                    q = ("sp", "du4%d" % (2 * h)) if qn[0] % 2 == 0 else ("pool", "du4%d" % (2 * h + 1))
                    qn[0] += 1
                    P.dma(q[0], q[1], U4[32 * g:32 * g + 32, 4 * h:4 * h + 4, fb, 0:L],
                          Uc[32 * fb:32 * fb + 32, 4 * h:4 * h + 4, 8 * g:8 * g + L],
                          reads=["Uc%d" % c for c in range(4 * h, 4 * h + 4)], writes=["U4_%d_%d_%d" % (h, g, fb)])

        inproj(0, (0, 1))
        for c in range(KC):
            if c + 1 < KC:
                inproj(c + 1, (0, 1))
            evac_u(c)
            s2_pump()
            if c == 3:
                shift_copies(0)
        shift_copies(1)
        for c in range(KC):
            inproj(c, (2,))
            evac_gate(c)
            s2_pump()
        for c in range(KC):
            conv(c)
            s2_pump()
        stats_all()
        uck = ["Uc%d" % c for c in range(KC)]
        if halo:
            P.op("dve", lambda e: e.tensor_single_scalar(Uc[:, :, 2:32], Uc[:, :, W + 2:W + 32], flag[:, 0:1], ALU.mult),
                 reads=uck + ["flag"], writes=uck)
        else:
            P.op("dve", lambda e: e.tensor_copy(Uc[:, :, 2:32], Uc[:, :, W + 2:W + 32]), reads=uck, writes=uck)
        if fin is not None:
            P.pe([lambda e, c=c: e.transpose(pbig[0:32, 2560 + c * 128:2560 + (c + 1) * 128], ufin[:, c, :], idf[:])
                  for c in range(KC)], reads=["ufin", "idf"], writes=psk(2560, 3584))
            P.op("dve", lambda e: e.tensor_copy(ytmp[0:32, :], pbig[0:32, 2560:3584]), reads=psk(2560, 3584), writes=["ytmp"])
            P.dma("sp", "dout", convo[fin], ytmp[0:32, :], reads=["ytmp"])

        _chk('t%d_s2' % ti)
        scope("T%d_s3" % ti)
        st1 = pbig[:, 2048 + c0:2048 + W]
        st2 = pbig[:, 2304 + c0:2304 + W]
        stk = psk(2048, 2560)
        P.op("act", lambda e: e.activation(lnv[:, c0:W], st1, AF.Square), reads=stk, writes=["lnv"])
        P.op("dve", lambda e: e.tensor_tensor(lnA[:, c0:W], st2, lnv[:, c0:W], ALU.subtract), reads=stk + ["lnv"],
             writes=["lnA"])
        P.op("act", lambda e: e.activation(lnv[:, c0:W], lnA[:, c0:W], AF.Ln, bias=epsc[:, 0:1]), reads=["lnA", "epsc"],
             writes=["lnv"])
        P.op("act", lambda e: e.activation(lnA[:, c0:W], lnv[:, c0:W], AF.Exp, scale=-0.5), reads=["lnv"], writes=["lnA"])
        P.op("dve", lambda e: e.scalar_tensor_tensor(lnB[:, c0:W], st1, -1.0, lnA[:, c0:W], ALU.mult, ALU.mult),
             reads=stk + ["lnA"], writes=["lnB"])
        def ln_front(c):
            i = c % 4
            P.op("pool", lambda e: e.tensor_tensor(lnt4[i][:, c0:W], cb[:, c, c0:W], lnA[:, c0:W], ALU.mult),
                 reads=["cb%d" % c, "lnA"], writes=["lntq%d" % i])
            P.op("dve", lambda e: e.tensor_tensor(lnt4[i][:, c0:W], lnt4[i][:, c0:W], lnB[:, c0:W], ALU.add),
                 reads=["lntq%d" % i, "lnB"], writes=["lntq%d" % i])
            P.op("act", lambda e: e.activation(lns4[i][:, c0:W], lnt4[i][:, c0:W], AF.Silu,
                                               bias=sp_[:, LNB + c:LNB + c + 1], scale=sp_[:, LNG + c:LNG + c + 1]),
                 reads=["lntq%d" % i, "smallp"], writes=["lnsq%d" % i] + (["th0"] if i >= 2 else []))

        def ln_back(c):
            i = c % 4
            P.op("dve", lambda e: e.tensor_tensor(cs[:, c, c0:W], lns4[i][:, c0:W], sgate[:, c, c0:W], ALU.mult),
                 reads=["lnsq%d" % i, "sgate%d" % c, "hT"] + (["th0"] if i >= 2 else []), writes=["cs%d" % c])
            P.pe([lambda e, b=b, nh=nh: e.matmul(pbig[:, 1024 * b + nh * 512:1024 * b + (nh + 1) * 512],
                                                 cs[:, c, b * 128:(b + 1) * 128], Wout0[:, c, nh * 512:(nh + 1) * 512],
                                                 start=(c == 0), stop=False) for b in blocks for nh in range(2)],
                 reads=["cs%d" % c, "Wout0"], writes=psk(0, 1024 * nb))

        LA = 3
        for c in range(LA):
            ln_front(c)
        for c in range(KC):
            if c + LA < KC:
                ln_front(c + LA)
            ln_back(c)
            if c % 2 == 1:
                bg_pump(1)
        P.pe([lambda e, b=b, nh=nh: e.matmul(pbig[:, 1024 * b + nh * 512:1024 * b + (nh + 1) * 512], ones[0:1, 0:128],
                                             borow[0:1, nh * 512:(nh + 1) * 512], start=False, stop=True)
              for b in blocks for nh in range(2)], reads=["borow", "ones"], writes=psk(0, 1024 * nb))

        _chk('t%d_s3' % ti)
        scope("T%d_s45" % ti)
        for b in blocks:
            post_norm_residual(xslot_of[(ti, b)], 0, 1024 * b)
        rs_ = {b: norm_sq(xslot_of[(ti, b)]) for b in blocks}
        hi_ = {b: norm_scale(xslot_of[(ti, b)], rs_[b]) for b in blocks}
        for b in blocks:
            norm_back(hi_[b], n1T, "n1T", b)
        pkv, pkk = pjc(c0, W - c0)
        P.pe([lambda e, kc=kc: e.matmul(pkv, Wkv[:, kc, 0:128], n1T[:, kc, c0:W], start=(kc == 0), stop=(kc == KC - 1))
              for kc in range(KC)], reads=["n1T", "Wkv"], writes=pkk)
        P.op("act", lambda e: e.activation(kT[:, 128 + c0:128 + W], pkv, AF.Identity), reads=pkk, writes=["kT"])
        for b in blocks:
            pvv, pvk = pjc(1024 + 512 * b, 128)
            P.pe([lambda e, kc=kc: e.matmul(pvv, n1T[:, kc, b * 128:(b + 1) * 128], Wkv[:, kc, 128:256],
                                            start=(kc == 0), stop=(kc == KC - 1)) for kc in range(KC)],
                 reads=["n1T", "Wkv"], writes=pvk)
            vk = "Vx%d" % (1 + b)
            pv3 = pvv.rearrange("p (k d) -> p k d", k=2)
            if halo:
                P.op("dve", lambda e: e.tensor_single_scalar(Vx[:, 1 + b, :, 0:64], pv3, flag[:, 0:1], ALU.mult),
                     reads=pvk + ["flag"], writes=[vk])
                P.op("dve", lambda e: e.tensor_copy(Vx[:, 1 + b, :, 64:128],
                                                    flag[:, 0:1].unsqueeze(2).to_broadcast([128, 2, 64])),
                     reads=["flag", vk], writes=[vk])
            else:
                P.op("dve", lambda e: e.tensor_copy(Vx[:, 1 + b, :, 0:64], pv3), reads=pvk, writes=[vk])
                P.op("dve", lambda e: e.memset(Vx[:, 1 + b, :, 64:128], 1.0), reads=[vk], writes=[vk])
        _chk('t%d_s45' % ti)
        P.barrier(skip=NOARENA)

        if fin is not None:
            b = nb - 1
            pvv, pvk = pjc(1024 + 512 * b, 128)
            P.op("dve", lambda e: e.tensor_copy(vtok[:], pvv), reads=pvk, writes=["vtok"])
            pkt, pktk = pjc(512, 128)
            P.pe([lambda e, kc=kc: e.matmul(pkt, n1T[:, kc, b * 128:(b + 1) * 128], Wkv[:, kc, 0:128],
                                            start=(kc == 0), stop=(kc == KC - 1)) for kc in range(KC)],
                 reads=["n1T", "Wkv"], writes=pktk)
            P.op("dve", lambda e: e.tensor_copy(ktok[:], pkt), reads=pktk, writes=["ktok"])
            if fin == 0:
                P.dma("sp", "dout", kvo[0], ktok[:], reads=["ktok"])
                P.dma("sp", "dout", kvo[1], vtok[:], reads=["vtok"])
            else:
                P.dma("sp", "dout", kvo[2, 64:128, :], ktok[0:64, :], reads=["ktok"])
                P.dma("sp", "dout", kvo[3, 64:128, :], vtok[0:64, :], reads=["vtok"])

        if tl["l1"]:
            bg_pump(len(bg_list))
            r0 = tl["ring0"]
            scope("T%d_s6" % ti)
            if ti + 1 < len(tiles):
                stage1_front(ti + 1)
            for oc in range(16):
                gi = r0 + oc // 2
                ring_ensure(gi + 2)
                sl = gi % 4
                pv, pk = pjc(512 * (oc % 4), W)
                P.pe([lambda e, kc=kc: e.matmul(pv, ring[sl][:, kc, (oc % 2) * 128:(oc % 2) * 128 + 128], n1T[:, kc, 0:W],
                                                start=(kc == 0), stop=(kc == KC - 1)) for kc in range(KC)],
                     reads=["n1T", "ring%d" % sl], writes=pk)
                if oc < 8:
                    P.op("dve", lambda e: e.tensor_copy(qT[:, oc, 0:W], pv), reads=pk, writes=["qT"])
                else:
                    P.op("act", lambda e: e.activation(sg2[:, oc - 8, 0:W], pv, AF.Silu), reads=pk, writes=["sg2"])
            if ti + 1 < len(tiles):
                stage1_back(ti + 1)
            _chk('t%d_s6' % ti)
            scope("T%d_s78" % ti)
            def att_scores_step(b, idx):
                kv, blk = idx // 2, idx % 2
                s0 = (idx % 2) * 1024
                S3 = pbig[:, s0:s0 + 1024].rearrange("p (g q) -> p g q", g=8)
                sk = psk(s0, s0 + 1024)
                kcols = kT[64 * kv:64 * kv + 64, blk * 128 + b * 128: blk * 128 + b * 128 + 128]
                fns = []
                for hf in range(2):
                    fns.append(lambda e, hf=hf: e.matmul(S3[:, hf * 4:(hf + 1) * 4, :], idb[:],
                                                         biasT[blk][:, kv * 8 + hf * 4:kv * 8 + hf * 4 + 4, :],
                                                         start=True, stop=False))
                for g in range(8):
                    fns.append(lambda e, g=g: e.matmul(S3[:, g, :], kcols, qT[64 * kv:64 * kv + 64, g, b * 128:(b + 1) * 128],
                                                       start=False, stop=(g % 4 == 3)))
                P.pe(fns, reads=["kT", "qT", "idb", "biasA", "biasB"], writes=sk)
                for hf in range(2):
                    P.op("act", lambda e, hf=hf: e.activation(PT[b % 2][idx][:, hf * 4:(hf + 1) * 4, :],
                                                              S3[:, hf * 4:(hf + 1) * 4, :], AF.Exp),
                         reads=sk, writes=["PT%d_%d" % (b % 2, idx)])

            def att_pv_step(b, jp):
                if True:
                    pb0 = 2048 if jp % 2 == 0 else (0 if b == nb - 1 else 2560)
                    o3 = pbig[:, pb0:pb0 + 256].rearrange("p (j q) -> p j q", j=2)
                    d3 = pbig[:, pb0 + 256:pb0 + 512].rearrange("p (j q) -> p j q", j=2)
                    pok = psk(pb0, pb0 + 512)
                    fns = []
                    for jj in range(2):
                        j = 2 * jp + jj
                        for kv in range(2):
                            for (c0_, v0_) in ((pb0, 0), (pb0 + 256, 64)):
                                for blk in range(2):
                                    fns.append(lambda e, jj=jj, j=j, kv=kv, blk=blk, c0_=c0_, v0_=v0_: e.matmul(
                                        pbig[64 * kv:64 * kv + 64, c0_ + jj * 128:c0_ + (jj + 1) * 128],
                                        Vx[:, b + blk, kv, v0_:v0_ + 64], PT[b % 2][kv * 2 + blk][:, j, :],
                                        start=(blk == 0), stop=(blk == 1), tile_position=(0, 64 * kv)))
                    P.pe(fns, reads=["Vx%d" % b, "Vx%d" % (b + 1)] + ["PT%d_%d" % (b % 2, i) for i in range(4)], writes=pok)
                    rb = jp % 2
                    for jj in range(2):
                        P.op("act", lambda e, jj=jj: e.activation(dsum[:, jj, :], d3[:, jj, :], AF.Ln,
                                                                  bias=spd[:, ESK + 2 * jp + jj:ESK + 2 * jp + jj + 1]),
                             reads=pok + ["spd_d"], writes=["dsum"])
                    P.op("act", lambda e: e.activation(rdenb[rb][:], dsum[:], AF.Exp, scale=-1.0), reads=["dsum"],
                         writes=["rden%d" % rb])
                    P.op("dve", lambda e: e.tensor_tensor(onrmb[rb][:], o3, rdenb[rb][:], ALU.mult),
                         reads=pok + ["rden%d" % rb], writes=["onrm%d" % rb])
                    P.op("pool", lambda e: e.tensor_tensor(og[:, 2 * jp:2 * jp + 2, b * 128:(b + 1) * 128], onrmb[rb][:],
                                                           sg2[:, 2 * jp:2 * jp + 2, b * 128:(b + 1) * 128], ALU.mult),
                         reads=["onrm%d" % rb, "sg2"], writes=["og%d" % b])

            def out_proj1_mm(b, ycol, nqs=(0, 1, 2, 3)):
                for nq in nqs:
                    gi = r0 + 8 + 4 * b + nq
                    ring_ensure(gi + 2)
                    sl = gi % 4
                    P.pe([lambda e, kc=kc: e.matmul(pbig[:, ycol + nq * 256:ycol + (nq + 1) * 256],
                                                    og[:, kc, b * 128:(b + 1) * 128], ring[sl][:, kc, :],
                                                    start=(kc == 0), stop=(kc == KC - 1)) for kc in range(KC)],
                         reads=["og%d" % b, "ring%d" % sl], writes=psk(ycol + nq * 256, ycol + (nq + 1) * 256))

            def out_proj1_post(b, ycol):
                xsl = xslot_of[(ti, b)]
                post_norm_residual(xsl, 1, ycol)
                nrow = 128 if tl["kind"] == "prompt" else 64
                P.dma("sp", "dout", yout[tl["yrow"] + b * 128: tl["yrow"] + b * 128 + nrow, :], xs[xsl][0:nrow, :],
                      reads=["xs%d" % xsl])

            for idx in range(4):
                att_scores_step(0, idx)
            ycols = [2560, 1024]
            for b in range(nb):
                for jp in range(4):
                    att_pv_step(b, jp)
                    if b + 1 < nb:
                        att_scores_step(b + 1, jp)
                    elif nb > 1:
                        out_proj1_mm(b - 1, ycols[b - 1], (jp,))
            out_proj1_mm(nb - 1, ycols[nb - 1])
            out_proj1_post(0, ycols[0])
            for b in range(1, nb):
                out_proj1_post(b, ycols[b])
            if ti + 1 < len(tiles) and tiles[ti + 1]["l1"]:
                ring_ensure(tiles[ti + 1]["ring0"] + 2)
        _chk('t%d_s8' % ti)
        scope("T%d_end" % ti)
        if tl["kind"] != "sample":
            P.op("dve", lambda e: e.tensor_copy(kT[:, 0:128], kT[:, W:W + 128]), reads=["kT"], writes=["kT"])
            P.op("dve", lambda e: e.tensor_copy(Vx[:, 0], Vx[:, nb]), reads=["Vx%d" % nb], writes=["Vx0"])
        P.barrier(skip=NOARENA)

    if not stopped:
        try:
            run_tiles()
        except _Stop:
            pass
    scope(None)
    P.barrier()
    P.final_wait("sp", [k for k in P.sem if P.isdma[k]])
    return nc


_CACHE = {}


def kernel(**inputs):
    f32 = np.float32
    g = lambda k: np.asarray(inputs[k], dtype=f32)
    x_prompt = g("x_prompt")[0]
    x_sample = g("x_sample")
    state_conv = g("state_conv")[0]
    cache_k = g("cache_k").reshape(8, 128, 128)
    cache_v = g("cache_v").reshape(8, 128, 128)
    a_b_in = g("a_b_in")[0]
    perm = np.concatenate([np.concatenate([np.arange(j * 64, j * 64 + 64), np.arange((8 + j) * 64, (8 + j) * 64 + 64)])
                           for j in range(8)])
    w_in1 = g("b_w_in")[0]
    w_in1p = np.ascontiguousarray(np.concatenate([w_in1[:, perm], w_in1[:, 1024 + perm]], axis=1))
    w_out1p = np.ascontiguousarray(g("b_w_out")[0][perm, :])
    fm = lambda v: np.ascontiguousarray(v.reshape(-1, 128).T)
    sinks = g("b_sinks")[0]
    sinks_fm = np.concatenate([np.tile(sinks[None, 0:8], (64, 1)), np.tile(sinks[None, 8:16], (64, 1))], axis=0)
    smallp = np.zeros((128, 96), f32)
    smallp[:, 0:24] = fm(a_b_in)
    smallp[:, 24:32] = fm(g("a_b_dw")[0])
    smallp[:, 32:40] = fm(g("a_ln_g")[0])
    smallp[:, 40:48] = fm(g("a_ln_b")[0])
    smallp[:, 48:56] = fm(g("a_pre_g")[0])
    smallp[:, 56:64] = fm(g("kv_g"))
    smallp[:, 64:72] = fm(g("b_pre_g")[0])
    smallp[:, 72:80] = sinks_fm
    wdw = g("a_w_dw")[0]
    wpad = np.concatenate([wdw, np.zeros((1, 1024), f32)], axis=0).reshape(4, 8, 32, 32)
    wdw_st = np.ascontiguousarray(wpad.transpose(0, 3, 2, 1).reshape(128, 256))
    rows = np.stack([a_b_in[0:1024], g("a_b_out")[0]], axis=0)
    pgb = np.stack([np.tile(g("a_post_g")[0][None, :], (128, 1)), np.tile(g("b_post_g")[0][None, :], (128, 1))], axis=0)
    common = dict(w_in0=np.ascontiguousarray(g("a_w_in")[0]), w_out0=np.ascontiguousarray(g("a_w_out")[0]),
                  w_kv=np.ascontiguousarray(g("w_kv")), w_in1p=w_in1p, w_out1p=w_out1p, smallp=smallp,
                  wdw_st=wdw_st, rows=np.ascontiguousarray(rows), pg=np.ascontiguousarray(pgb))
    in_maps = []
    for i in range(NCORES):
        xin = np.zeros((XROWS, D), f32)
        if i > 0:
            xin[0:256] = x_prompt[2048 * i - 256:2048 * i]
        xin[256:2304] = x_prompt[2048 * i:2048 * (i + 1)]
        xin[2304:2368] = x_sample[i]
        hist = np.zeros((32, D), f32)
        hist[2:32] = state_conv[i]
        m = dict(common)
        m.update(xin=xin, hist=hist, ck=np.ascontiguousarray(cache_k[i]), cv=np.ascontiguousarray(cache_v[i]),
                 flag=np.full((128, 1), 0.0 if i == 0 else 1.0, f32))
        in_maps.append(m)
    if "nc" not in _CACHE:
        _CACHE["nc"] = build_program()
    res = run_bass_kernel_spmd(_CACHE["nc"], in_maps, core_ids=list(range(NCORES)))
    R = res.results
    y_prompt = np.concatenate([R[i]["yout"][0:2048] for i in range(NCORES)], axis=0)[None].astype(f32)
    y_sample = np.stack([R[i]["yout"][2048:2112] for i in range(NCORES)], axis=0).astype(f32)
    conv_p = R[NCORES - 1]["convo"][0][2:32][None, None].astype(f32)
    conv_s = np.stack([R[i]["convo"][1][2:32] for i in range(NCORES)], axis=0)[None].astype(f32)
    k_p = R[NCORES - 1]["kvo"][0].reshape(1, 128, 2, 64).astype(f32)
    v_p = R[NCORES - 1]["kvo"][1].reshape(1, 128, 2, 64).astype(f32)
    k_s = np.stack([R[i]["kvo"][2].reshape(128, 2, 64) for i in range(NCORES)], axis=0).astype(f32)
    v_s = np.stack([R[i]["kvo"][3].reshape(128, 2, 64) for i in range(NCORES)], axis=0).astype(f32)
    return (y_prompt, y_sample, conv_p, conv_s, k_p, v_p, k_s, v_s)
```
